# Optimizing a Trainium2 kernel written in Bass

```python
import math
import jax
import jax.numpy as jnp
from jax import lax
import numpy as np

D_MODEL = 1024
BATCH = 4
SEQ = 8192
DEPTH = 4

N_AB = (DEPTH + 1) // 2
N_CD = DEPTH // 2

SSD_HEADS = 8
SSD_HEAD_DIM = 64
SSD_D_INNER = SSD_HEADS * SSD_HEAD_DIM
SSD_GROUPS = 2
SSD_D_STATE = 64
SSD_CONV = 4
SSD_CHUNK = 128
SSD_CONV_DIM = SSD_D_INNER + 2 * SSD_GROUPS * SSD_D_STATE

HG_HEADS = 4
HG_KEY_DIM = 128
HG_VAL_DIM = 128
HG_WIDTH = HG_HEADS * HG_KEY_DIM
HG_CHUNK = 64

SWA_Q_HEADS = 8
SWA_KV_HEADS = 2
SWA_HEAD_DIM = 64
SWA_WINDOW = 128
SWA_BLOCK = 128

RG_WIDTH = 512
RG_BLOCKS = 8
RG_BLOCK_DIM = RG_WIDTH // RG_BLOCKS
RG_CONV = 4
RG_C = 8.0

FFN_DIM = 2816
FFN_CONV = 3

LN_EPS = 1e-5
RMS_EPS = 1e-6
MASK_VALUE = -1e9
ALPHA = (2 * DEPTH) ** 0.25
BETA = (8 * DEPTH) ** -0.25

AB_SIZES = (SSD_D_INNER, SSD_CONV_DIM, SSD_HEADS, HG_WIDTH, HG_WIDTH, HG_HEADS * HG_VAL_DIM, HG_HEADS * HG_VAL_DIM)
AB_IN = sum(AB_SIZES)
AB_OUT_IN = SSD_D_INNER + HG_HEADS * HG_VAL_DIM
CD_SIZES = (SWA_Q_HEADS * SWA_HEAD_DIM, SWA_KV_HEADS * SWA_HEAD_DIM, SWA_KV_HEADS * SWA_HEAD_DIM, RG_WIDTH, RG_WIDTH)
CD_IN = sum(CD_SIZES)
CD_OUT_IN = SWA_Q_HEADS * SWA_HEAD_DIM + RG_WIDTH

kernel_name = 'hybrid_ssd_hgrn2_swa_rglru_deepnorm'


def _layer_norm(x, g, b):
    xf = x.astype(jnp.float32)
    mu = jnp.mean(xf, -1, keepdims=True)
    var = jnp.mean(jnp.square(xf - mu), -1, keepdims=True)
    return ((xf - mu) * lax.rsqrt(var + LN_EPS) * g.astype(jnp.float32) + b.astype(jnp.float32)).astype(x.dtype)


def _rms_norm(x, w):
    xf = x.astype(jnp.float32)
    return xf * lax.rsqrt(jnp.mean(jnp.square(xf), -1, keepdims=True) + RMS_EPS) * w.astype(jnp.float32)


def _split(h, sizes):
    return jnp.split(h, np.cumsum(sizes)[:-1].tolist(), axis=-1)


def _causal_dwconv(x, w, b):
    width = w.shape[0]
    y = lax.conv_general_dilated(x, w[:, None, :].astype(x.dtype), window_strides=(1,),
                                 padding=((width - 1, 0),), dimension_numbers=('NWC', 'WIO', 'NWC'),
                                 feature_group_count=x.shape[-1])
    return y + b.astype(x.dtype)


def _masked_exp(diff, mask):
    return jnp.where(mask, jnp.exp(jnp.where(mask, diff, 0.0)), 0.0)


def _decay_matrix(cs):
    t = cs.shape[-1]
    mask = jnp.tril(jnp.ones((t, t), dtype=bool))
    return _masked_exp(cs[..., :, None] - cs[..., None, :], mask)


def _ssd_scan(x, dt, a, bm, cm):
    b, s, h, p = x.shape
    g, n = bm.shape[-2:]
    L = SSD_CHUNK
    nc = s // L
    rep = h // g
    bh = jnp.repeat(bm, rep, axis=2).reshape(b, nc, L, h, n)
    ch = jnp.repeat(cm, rep, axis=2).reshape(b, nc, L, h, n)
    xc = (x * dt[..., None]).reshape(b, nc, L, h, p)
    a_cs = jnp.cumsum((dt * a).reshape(b, nc, L, h).transpose(0, 3, 1, 2), axis=-1)
    scores = jnp.einsum('bclhn,bcshn->bhcls', ch, bh) * _decay_matrix(a_cs)
    y_diag = jnp.einsum('bhcls,bcshp->bclhp', scores, xc)
    decay_states = jnp.exp(a_cs[..., -1:] - a_cs)
    states = jnp.einsum('bclhn,bhcl,bclhp->bchpn', bh, decay_states, xc)
    states = jnp.concatenate([jnp.zeros_like(states[:, :1]), states], axis=1)
    chunk_cs = jnp.cumsum(jnp.pad(a_cs[..., -1], ((0, 0), (0, 0), (1, 0))), axis=-1)
    states = jnp.einsum('bhzc,bchpn->bzhpn', _decay_matrix(chunk_cs), states)[:, :-1]
    y_off = jnp.einsum('bclhn,bchpn,bhcl->bclhp', ch, states, jnp.exp(a_cs))
    return (y_diag + y_off).reshape(b, s, h, p)


def _ssd_mixer(z, xbc, dt_raw, conv_w, conv_b, dt_bias, a_log, d_skip, norm_w):
    bsz, seq, _ = z.shape
    xbc = jax.nn.silu(_causal_dwconv(xbc, conv_w, conv_b)).astype(jnp.float32)
    xs, bm, cm = _split(xbc, (SSD_D_INNER, SSD_GROUPS * SSD_D_STATE, SSD_GROUPS * SSD_D_STATE))
    xs = xs.reshape(bsz, seq, SSD_HEADS, SSD_HEAD_DIM)
    bm = bm.reshape(bsz, seq, SSD_GROUPS, SSD_D_STATE)
    cm = cm.reshape(bsz, seq, SSD_GROUPS, SSD_D_STATE)
    dt = jax.nn.softplus(dt_raw.astype(jnp.float32) + dt_bias.astype(jnp.float32))
    a = -jnp.exp(a_log.astype(jnp.float32))
    y = _ssd_scan(xs, dt, a, bm, cm) + d_skip.astype(jnp.float32)[:, None] * xs
    y = y.reshape(bsz, seq, SSD_D_INNER) * jax.nn.silu(z.astype(jnp.float32))
    y = _rms_norm(y.reshape(bsz, seq, SSD_GROUPS, -1), norm_w.reshape(SSD_GROUPS, -1))
    return y.reshape(bsz, seq, SSD_D_INNER)


def _hgrn2_scan(q, k, v, log_f):
    b, s, h, dk = q.shape
    dv = v.shape[-1]
    L = HG_CHUNK
    nc = s // L
    to_chunks = lambda t: t.reshape(b, nc, L, h, t.shape[-1]).transpose(1, 0, 3, 2, 4)
    mask = jnp.tril(jnp.ones((L, L), dtype=bool))[:, :, None]

    def step(state, inp):
        qc, kc, vc, gc = inp
        bc = jnp.cumsum(gc, axis=2)
        decay = _masked_exp(bc[:, :, :, None, :] - bc[:, :, None, :, :], mask)
        attn = jnp.einsum('bhtk,bhsk,bhtsk->bhts', qc, kc, decay)
        out = jnp.einsum('bhts,bhsv->bhtv', attn, vc) + jnp.einsum('bhtk,bhkv->bhtv', qc * jnp.exp(bc), state)
        b_last = bc[:, :, -1:, :]
        state = jnp.exp(b_last[:, :, 0])[..., None] * state + jnp.einsum('bhsk,bhsv->bhkv', kc * jnp.exp(b_last - bc), vc)
        return state, out

    state0 = jnp.zeros((b, h, dk, dv), jnp.float32)
    _, out = lax.scan(step, state0, (to_chunks(q), to_chunks(k), to_chunks(v), to_chunks(log_f)))
    return out.transpose(1, 0, 3, 2, 4).reshape(b, s, h, dv)


def _hgrn2_mixer(hq, hf, hi, hg, lb, norm_w):
    bsz, seq, _ = hq.shape
    q = jax.nn.silu(hq.astype(jnp.float32)).reshape(bsz, seq, HG_HEADS, HG_KEY_DIM)
    fx = hf.astype(jnp.float32).reshape(bsz, seq, HG_HEADS, HG_KEY_DIM)
    lb = lb.reshape(HG_HEADS, HG_KEY_DIM)
    log_f = jnp.log(lb + (1.0 - lb) * jax.nn.sigmoid(fx))
    k = (1.0 - lb) * jax.nn.sigmoid(-fx)
    v = hi.astype(jnp.float32).reshape(bsz, seq, HG_HEADS, HG_VAL_DIM)
    o = _hgrn2_scan(q, k, v, log_f)
    o = _rms_norm(o, norm_w) * jax.nn.silu(hg.astype(jnp.float32).reshape(bsz, seq, HG_HEADS, HG_VAL_DIM))
    return o.reshape(bsz, seq, HG_HEADS * HG_VAL_DIM)


def _swa_sink_attention(q, k, v, sinks):
    b, s, hq, d = q.shape
    hkv = k.shape[2]
    grp = hq // hkv
    T = SWA_BLOCK
    nb = s // T
    qb = q.reshape(b, nb, T, hkv, grp, d)

    def banded(t):
        tb = t.reshape(b, nb, T, hkv, d)
        prev = jnp.pad(tb, ((0, 0), (1, 0), (0, 0), (0, 0), (0, 0)))[:, :-1]
        return jnp.concatenate([prev, tb], axis=2)

    kb, vb = banded(k), banded(v)
    scores = jnp.einsum('bnqhgd,bnkhd->bnhgqk', qb, kb).astype(jnp.float32) * (d ** -0.5)
    rel = (jnp.arange(T)[:, None] + T) - jnp.arange(2 * T)[None, :]
    band = (rel >= 0) & (rel < SWA_WINDOW)
    valid = (jnp.arange(nb)[:, None] > 0) | (jnp.arange(2 * T)[None, :] >= T)
    mask = band[None] & valid[:, None, :]
    scores = jnp.where(mask[None, :, None, None], scores, MASK_VALUE)
    sink = sinks.astype(jnp.float32).reshape(hkv, grp)
    sink_col = jnp.broadcast_to(sink[None, None, :, :, None, None], scores.shape[:-1] + (1,))
    probs = jax.nn.softmax(jnp.concatenate([scores, sink_col], axis=-1), axis=-1)[..., :-1]
    out = jnp.einsum('bnhgqk,bnkhd->bnqhgd', probs.astype(v.dtype), vb)
    return out.reshape(b, s, hq * d)


def _rglru_mixer(gate, xr, conv_w, conv_b, wa, ba, wx, bx, lam):
    bsz, seq, _ = xr.shape
    xc = _causal_dwconv(xr, conv_w, conv_b).astype(jnp.float32)
    xblk = xc.reshape(bsz, seq, RG_BLOCKS, RG_BLOCK_DIM)
    r = jax.nn.sigmoid(jnp.einsum('bsgi,gij->bsgj', xblk, wa.astype(jnp.float32)).reshape(bsz, seq, RG_WIDTH) + ba.astype(jnp.float32))
    i = jax.nn.sigmoid(jnp.einsum('bsgi,gij->bsgj', xblk, wx.astype(jnp.float32)).reshape(bsz, seq, RG_WIDTH) + bx.astype(jnp.float32))
    log_a = -RG_C * r * jax.nn.softplus(-lam.astype(jnp.float32))
    a = jnp.exp(log_a)
    u = jnp.sqrt(jnp.maximum(-jnp.expm1(2.0 * log_a), 0.0)) * (i * xc)

    def combine(c1, c2):
        a1, u1 = c1
        a2, u2 = c2
        return a1 * a2, a2 * u1 + u2

    _, h = lax.associative_scan(combine, (a, u), axis=1)
    return h * jax.nn.gelu(gate.astype(jnp.float32))


def _ab_mixer(x, w_in, conv_w, conv_b, dt_bias, a_log, d_skip, ssd_norm_w, lb, hg_norm_w, w_out):
    h = x @ w_in
    z, xbc, dt_raw, hq, hf, hi, hg = _split(h, AB_SIZES)
    y_a = _ssd_mixer(z, xbc, dt_raw, conv_w, conv_b, dt_bias, a_log, d_skip, ssd_norm_w)
    y_b = _hgrn2_mixer(hq, hf, hi, hg, lb, hg_norm_w)
    return jnp.concatenate([y_a, y_b], axis=-1).astype(x.dtype) @ w_out


def _cd_mixer(x, w_in, sinks, conv_w, conv_b, wa, ba, wx, bx, lam, w_out):
    bsz, seq, _ = x.shape
    h = x @ w_in
    q, k, v, gate, xr = _split(h, CD_SIZES)
    y_c = _swa_sink_attention(q.reshape(bsz, seq, SWA_Q_HEADS, SWA_HEAD_DIM),
                              k.reshape(bsz, seq, SWA_KV_HEADS, SWA_HEAD_DIM),
                              v.reshape(bsz, seq, SWA_KV_HEADS, SWA_HEAD_DIM), sinks)
    y_d = _rglru_mixer(gate, xr, conv_w, conv_b, wa, ba, wx, bx, lam)
    return jnp.concatenate([y_c.astype(jnp.float32), y_d], axis=-1).astype(x.dtype) @ w_out


def _conv_ffn(x, w_up, conv_w, conv_b, w_down):
    h = _causal_dwconv(x @ w_up, conv_w, conv_b)
    g, u = jnp.split(h, 2, axis=-1)
    return (jax.nn.silu(g) * u) @ w_down


def setup_inputs(seed: int = 0) -> dict:
    key = jax.random.key(seed)
    keys = list(jax.random.split(key, 40))

    def nrm(shape, scale):
        return jax.random.normal(keys.pop(), shape, jnp.float32) * scale

    def unif(shape, lo, hi):
        return jax.random.uniform(keys.pop(), shape, jnp.float32, minval=lo, maxval=hi)

    x = nrm((BATCH, SEQ, D_MODEL), 1.0)
    ab_w_in = nrm((N_AB, D_MODEL, AB_IN), D_MODEL ** -0.5)
    ssd_conv_w = nrm((N_AB, SSD_CONV, SSD_CONV_DIM), SSD_CONV ** -0.5)
    ssd_conv_b = nrm((N_AB, SSD_CONV_DIM), 0.02)
    dt0 = jnp.exp(unif((N_AB, SSD_HEADS), math.log(1e-3), math.log(1e-1)))
    ssd_dt_bias = dt0 + jnp.log(-jnp.expm1(-dt0))
    ssd_a_log = jnp.log(unif((N_AB, SSD_HEADS), 1.0, 16.0))
    ssd_d = 1.0 + nrm((N_AB, SSD_HEADS), 0.1)
    ssd_norm_w = 1.0 + nrm((N_AB, SSD_D_INNER), 0.1)
    hg_lower = nrm((N_AB, HG_WIDTH), 1.0)
    hg_norm_w = 1.0 + nrm((N_AB, HG_VAL_DIM), 0.1)
    ab_w_out = nrm((N_AB, AB_OUT_IN, D_MODEL), (AB_OUT_IN ** -0.5) * BETA)
    cd_w_in = nrm((N_CD, D_MODEL, CD_IN), D_MODEL ** -0.5)
    swa_sinks = nrm((N_CD, SWA_Q_HEADS), 0.5)
    rg_conv_w = nrm((N_CD, RG_CONV, RG_WIDTH), RG_CONV ** -0.5)
    rg_conv_b = nrm((N_CD, RG_WIDTH), 0.02)
    rg_wa = nrm((N_CD, RG_BLOCKS, RG_BLOCK_DIM, RG_BLOCK_DIM), RG_BLOCK_DIM ** -0.5)
    rg_ba = nrm((N_CD, RG_WIDTH), 0.02)
    rg_wx = nrm((N_CD, RG_BLOCKS, RG_BLOCK_DIM, RG_BLOCK_DIM), RG_BLOCK_DIM ** -0.5)
    rg_bx = nrm((N_CD, RG_WIDTH), 0.02)
    sig = unif((N_CD, RG_WIDTH), 0.9, 0.999) ** (1.0 / RG_C)
    rg_lambda = jnp.log(sig) - jnp.log1p(-sig)
    cd_w_out = nrm((N_CD, CD_OUT_IN, D_MODEL), (CD_OUT_IN ** -0.5) * BETA)
    ffn_w_up = nrm((DEPTH, D_MODEL, 2 * FFN_DIM), D_MODEL ** -0.5)
    ffn_conv_w = nrm((DEPTH, FFN_CONV, 2 * FFN_DIM), FFN_CONV ** -0.5)
    ffn_conv_b = nrm((DEPTH, 2 * FFN_DIM), 0.02)
    ffn_w_down = nrm((DEPTH, FFN_DIM, D_MODEL), (FFN_DIM ** -0.5) * BETA)
    ln_g = 1.0 + nrm((DEPTH, 2, D_MODEL), 0.05)
    ln_b = nrm((DEPTH, 2, D_MODEL), 0.02)
    return {'x': x, 'ab_w_in': ab_w_in, 'ssd_conv_w': ssd_conv_w, 'ssd_conv_b': ssd_conv_b,
            'ssd_dt_bias': ssd_dt_bias, 'ssd_a_log': ssd_a_log, 'ssd_d': ssd_d, 'ssd_norm_w': ssd_norm_w,
            'hg_lower': hg_lower, 'hg_norm_w': hg_norm_w, 'ab_w_out': ab_w_out, 'cd_w_in': cd_w_in,
            'swa_sinks': swa_sinks, 'rg_conv_w': rg_conv_w, 'rg_conv_b': rg_conv_b, 'rg_wa': rg_wa,
            'rg_ba': rg_ba, 'rg_wx': rg_wx, 'rg_bx': rg_bx, 'rg_lambda': rg_lambda, 'cd_w_out': cd_w_out,
            'ffn_w_up': ffn_w_up, 'ffn_conv_w': ffn_conv_w, 'ffn_conv_b': ffn_conv_b, 'ffn_w_down': ffn_w_down,
            'ln_g': ln_g, 'ln_b': ln_b}


def reference(x, ab_w_in, ssd_conv_w, ssd_conv_b, ssd_dt_bias, ssd_a_log, ssd_d, ssd_norm_w,
              hg_lower, hg_norm_w, ab_w_out, cd_w_in, swa_sinks, rg_conv_w, rg_conv_b, rg_wa,
              rg_ba, rg_wx, rg_bx, rg_lambda, cd_w_out, ffn_w_up, ffn_conv_w, ffn_conv_b, ffn_w_down,
              ln_g, ln_b):
    sm = jax.nn.softmax(hg_lower.astype(jnp.float32), axis=0)
    lb_all = jnp.clip(jnp.cumsum(sm, axis=0) - sm[0], 0.0, 1.0)
    for layer in range(DEPTH):
        j = layer // 2
        if layer % 2 == 0:
            m = _ab_mixer(x, ab_w_in[j], ssd_conv_w[j], ssd_conv_b[j], ssd_dt_bias[j], ssd_a_log[j],
                          ssd_d[j], ssd_norm_w[j], lb_all[j], hg_norm_w[j], ab_w_out[j])
        else:
            m = _cd_mixer(x, cd_w_in[j], swa_sinks[j], rg_conv_w[j], rg_conv_b[j], rg_wa[j], rg_ba[j],
                          rg_wx[j], rg_bx[j], rg_lambda[j], cd_w_out[j])
        x = _layer_norm(ALPHA * x + m, ln_g[layer, 0], ln_b[layer, 0])
        f = _conv_ffn(x, ffn_w_up[layer], ffn_conv_w[layer], ffn_conv_b[layer], ffn_w_down[layer])
        x = _layer_norm(ALPHA * x + f, ln_g[layer, 1], ln_b[layer, 1])
    return x
```

```python
import math
from contextlib import ExitStack
import numpy as np
import concourse.bass as bass
import concourse.mybir as mybir
from concourse.bass_utils import run_bass_kernel_spmd

F32 = mybir.dt.float32
BF16 = mybir.dt.bfloat16
AF = mybir.ActivationFunctionType
ALU = mybir.AluOpType

D = 1024
DEPTH = 4
ALPHA = (2 * DEPTH) ** 0.25
FFN = 2816
TT = 512
LN_EPS = 1e-5
RMS_EPS = 1e-6
NEG = -30000.0


class Buf:
    def __init__(self, h=None, name=""):
        self.h = h
        self.name = name
        self.w = None
        self.r = {}

    def __getitem__(self, idx):
        return self.h[idx]


class AliasBuf(Buf):
    def __init__(self, parent, h, name=""):
        self.__dict__["parent"] = parent
        self.__dict__["h"] = h
        self.__dict__["name"] = name

    @property
    def w(self):
        return self.parent.w

    @w.setter
    def w(self, v):
        self.parent.w = v

    @property
    def r(self):
        return self.parent.r

    @r.setter
    def r(self, v):
        self.parent.r = v


class Sched:
    ENGS = ["pe", "act", "dve", "pool", "sp"]
    NDMASEM = 6

    def __init__(self, nc, stack):
        self.nc = nc
        self.q = {e: [] for e in self.ENGS}
        self.cnt = {e: 0 for e in self.ENGS}
        self.seen = {e: {} for e in self.ENGS}
        self.sems = {}
        for e in self.ENGS:
            self.sems[e] = stack.enter_context(nc.semaphore("s_" + e))
        self.dma_i = {}
        self.dma_val = {}
        for qn in ["sp", "act", "pool"]:
            self.dma_i[qn] = 0
            for k in range(self.NDMASEM):
                key = "d_%s_%d" % (qn, k)
                self.sems[key] = stack.enter_context(nc.semaphore(key))
                self.dma_val[key] = 0

    def _need(self, eng, tok, waits):
        if tok is None:
            return
        key, val = tok
        if self.seen[eng].get(key, 0) >= val:
            return
        for i, (k2, v2) in enumerate(waits):
            if k2 == key:
                if v2 < val:
                    waits[i] = (key, val)
                return
        waits.append((key, val))

    def _deps(self, eng, reads, writes, is_dma=False):
        waits = []
        for b in reads:
            if b.w is not None:
                if b.w[0] == eng and eng == "pe" and not is_dma:
                    continue
                self._need(eng, b.w, waits)
        for b in writes:
            if b.w is not None and not (b.w[0] == eng and not is_dma):
                self._need(eng, b.w, waits)
            for k, v in b.r.items():
                if k == eng and not is_dma:
                    continue
                self._need(eng, (k, v), waits)
        for k, v in waits:
            self.seen[eng][k] = v
        return waits

    def op(self, eng, fn, reads=(), writes=()):
        waits = self._deps(eng, reads, writes)
        self.cnt[eng] += 1
        tok = (eng, self.cnt[eng])
        self.q[eng].append((waits, fn, (eng, 1)))
        for b in reads:
            b.r[eng] = tok[1]
        for b in writes:
            b.w = tok
            b.r = {}
        return tok

    def dma(self, qn, fn, reads=(), writes=()):
        i = self.dma_i[qn]
        self.dma_i[qn] += 1
        key = "d_%s_%d" % (qn, i % self.NDMASEM)
        waits = self._deps(qn, reads, writes, is_dma=True)
        prev = self.dma_val[key]
        if prev > 0:
            self._need(qn, (key, prev), waits)
            self.seen[qn][key] = max(self.seen[qn].get(key, 0), prev)
        self.dma_val[key] = prev + 16
        tok = (key, prev + 16)
        self.q[qn].append((waits, fn, (key, 16)))
        for b in reads:
            b.r[key] = tok[1]
        for b in writes:
            b.w = tok
            b.r = {}
        return tok

    def wait_all(self, eng, toks):
        waits = []
        for t in toks:
            self._need(eng, t, waits)
        for k, v in waits:
            self.seen[eng][k] = v
        self.q[eng].append((waits, None, None))

    def barrier(self):
        toks = [(e, self.cnt[e]) for e in self.ENGS if self.cnt[e] > 0]
        toks += [(k, v) for k, v in self.dma_val.items() if v > 0]
        for e in self.ENGS:
            self.wait_all(e, toks)

    def emit(self, block):
        handles = {"pe": block.tensor, "act": block.scalar, "dve": block.vector,
                   "pool": block.gpsimd, "sp": block.sync}
        sems = self.sems
        for e in self.ENGS:
            def body(engine, q=self.q[e]):
                for waits, fn, inc in q:
                    for k, v in waits:
                        engine.wait_ge(sems[k], v)
                    if fn is not None:
                        fn(engine).then_inc(sems[inc[0]], inc[1])
            handles[e](body)
        self.q = {e: [] for e in self.ENGS}


class WRes:
    def __init__(self, h, kc, n):
        self.h = h
        self.kc = kc
        self.n = n
        self.pieces = []

    def bufs(self, lo, hi):
        return [b for (a, z, b) in self.pieces if a < hi and lo < z]


class Ctx:
    pass


def build_program(S, layer_list, phases=("mix", "ffn")):
    nc = bass.Bass("TRN2", target_bir_lowering=False)
    NT = S // TT
    c = Ctx()
    c.nc = nc
    c.S = S

    def din(name, shape):
        return nc.dram_tensor(name, list(shape), F32, kind="ExternalInput").ap()

    c.xT = din("xT", [8, 128, S])
    c.ab_w_in = din("ab_w_in", [2, 1024, 3336])
    c.ab_w_out = din("ab_w_out", [2, 1024, 1024])
    c.cd_w_in = din("cd_w_in", [2, 1024, 1792])
    c.cd_w_out = din("cd_w_out", [2, 1024, 1024])
    c.ffn_w_up = din("ffn_w_up", [4, 1024, 5632])
    c.ffn_w_down = din("ffn_w_down", [4, 2816, 1024])
    c.pp_in = din("pp", [128, PP_N])
    c.bc_in = din("bcp", [BC_N])
    c.rg_w = din("rg_w", [2, 2, 8, 64, 64])
    c.masks_in = din("masks", [128, 2, 4, 128])
    c.cst_in = din("cst", [128, 512])
    c.sel_in = din("sel", [8, 8, 128])
    c.outT = nc.dram_tensor("outT", [8, 128, S], F32, kind="ExternalOutput").ap()
    c.xs = nc.dram_tensor("xscratch", [8, 128, S], F32).ap()
    c.wdb_d = nc.dram_tensor("wdb_scratch", [8, 128, 22 * 128], BF16).ap()

    with ExitStack() as st:
        s = Sched(nc, st)
        c.s = s
        c.st = st

        def sb(name, shape, dt=F32):
            return Buf(st.enter_context(nc.sbuf_tensor("t_" + name, list(shape), dt)), name)
        c.sb = sb
        c.banks = [Buf(st.enter_context(nc.psum_tensor("bank%d" % i, [128, 512], F32)), "bank%d" % i)
                   for i in range(8)]
        c.pp = sb("pp", [128, PP_N])
        c.bcp = sb("bcp", [128, BC_N])
        s.dma("sp", lambda e: e.dma_start(out=c.pp[:], in_=c.pp_in), writes=[c.pp])
        s.dma("sp", lambda e: e.dma_start(out=c.bcp[:], in_=c.bc_in.partition_broadcast(128)), writes=[c.bcp])
        c.ones_ln = sb("ones_ln", [128, 128], BF16)
        s.op("dve", lambda e: e.memset(c.ones_ln[:], 1.0 / 1024), writes=[c.ones_ln])
        c.eps_ln = sb("eps_ln", [128, 1])
        s.op("dve", lambda e: e.memset(c.eps_ln[:], LN_EPS), writes=[c.eps_ln])
        c.lnw = [sb("lnw%d" % i, [128, 2, TT], BF16) for i in range(2)]
        c.lns = [sb("lns%d" % i, [128, TT]) for i in range(2)]
        c.xdram = [Buf(None, "xd%d" % t) for t in range(NT)]
        c.masks = sb("masks", [128, 2, 4, 128], BF16)
        s.dma("pool", lambda e: e.dma_start(out=c.masks[:], in_=c.masks_in), writes=[c.masks])
        c.ones64 = sb("ones64", [128, 64], BF16)
        s.op("dve", lambda e: e.memset(c.ones64[:], 1.0), writes=[c.ones64])
        c.first = True
        nph = len(layer_list) * len(phases)
        ip = 0
        for layer in layer_list:
            for ph in phases:
                ip += 1
                last = ip == nph
                src = c.xT if c.first else c.xs
                dst = c.outT if last else c.xs
                c.first = False
                with ExitStack() as pst:
                    def psb(name, shape, dt=F32, ip=ip, pst=pst):
                        return Buf(pst.enter_context(nc.sbuf_tensor("p%d_%s" % (ip, name), list(shape), dt)), name)
                    c.psb = psb
                    c.pst = pst
                    c.ip = ip
                    s.barrier()
                    if ph == "ffn":
                        toks = ffn_phase(c, layer, src, dst, NT)
                    elif layer % 2 == 0:
                        toks = ab_phase(c, layer, src, dst, NT)
                    else:
                        toks = cd_phase(c, layer, src, dst, NT)
                    if last:
                        s.wait_all("sp", toks)
                    with nc.Block() as block:
                        s.emit(block)
    return nc


PP_OFF = {}
PP_N = 0
BC_OFF = {}
BC_N = 0


def _pp_alloc(name, n):
    global PP_N
    PP_OFF[name] = (PP_N, n)
    PP_N += n


def _bc_alloc(name, n):
    global BC_N
    BC_OFF[name] = (BC_N, n)
    BC_N += n


for _l in range(4):
    _pp_alloc("ln_g%d" % _l, 16)
    _pp_alloc("ln_b%d" % _l, 16)
    _pp_alloc("fcw%d" % _l, 3 * 44)
    _pp_alloc("fcb%d" % _l, 44)
for _j in range(2):
    _pp_alloc("scw%d" % _j, 4 * 6)
    _pp_alloc("scb%d" % _j, 6)
    _pp_alloc("sd%d" % _j, 4)
    _pp_alloc("snw%d" % _j, 4)
    _pp_alloc("hgl%d" % _j, 4)
    _pp_alloc("hnw%d" % _j, 1)
    _pp_alloc("rcw%d" % _j, 4 * 4)
    _pp_alloc("rcb%d" % _j, 4)
    _pp_alloc("rba%d" % _j, 4)
    _pp_alloc("rbx%d" % _j, 4)
    _pp_alloc("rlam%d" % _j, 4)
    _pp_alloc("sinkpp%d" % _j, 4)
    _bc_alloc("dtb%d" % _j, 8)
    _bc_alloc("alog%d" % _j, 8)
    _bc_alloc("sink%d" % _j, 8)


def ppv(c, name, i=None):
    off, n = PP_OFF[name]
    if i is None:
        return c.pp[:, off:off + n]
    return c.pp[:, off + i:off + i + 1]


def pack_params(inp):
    pp = np.zeros((128, PP_N), np.float32)
    bc = np.zeros((BC_N,), np.float32)

    def put(name, arr):
        off, n = PP_OFF[name]
        a = np.asarray(arr, np.float32)
        lead = a.shape[:-1]
        nch = a.shape[-1] // 128
        a = a.reshape(lead + (nch, 128))
        a = np.moveaxis(a, -1, 0).reshape(128, -1)
        assert a.shape[1] == n, (name, a.shape, n)
        pp[:, off:off + n] = a

    for l in range(4):
        put("ln_g%d" % l, inp["ln_g"][l])
        put("ln_b%d" % l, inp["ln_b"][l])
        put("fcw%d" % l, inp["ffn_conv_w"][l])
        put("fcb%d" % l, inp["ffn_conv_b"][l])
    for j in range(2):
        put("scw%d" % j, inp["ssd_conv_w"][j])
        put("scb%d" % j, inp["ssd_conv_b"][j])
        put("sd%d" % j, np.repeat(np.asarray(inp["ssd_d"][j]), 64))
        put("snw%d" % j, inp["ssd_norm_w"][j])
        put("hgl%d" % j, inp["hg_lower"][j])
        put("hnw%d" % j, inp["hg_norm_w"][j])
        put("rcw%d" % j, inp["rg_conv_w"][j])
        put("rcb%d" % j, inp["rg_conv_b"][j])
        put("rba%d" % j, inp["rg_ba"][j])
        put("rbx%d" % j, inp["rg_bx"][j])
        put("rlam%d" % j, inp["rg_lambda"][j])
        sk = np.asarray(inp["swa_sinks"][j], np.float32)
        off, n = PP_OFF["sinkpp%d" % j]
        for g in range(2):
            for jj in range(2):
                pp[0:64, off + g * 2 + jj] = sk[4 * g + 2 * jj]
                pp[64:128, off + g * 2 + jj] = sk[4 * g + 2 * jj + 1]
        for nm, key in (("dtb", "ssd_dt_bias"), ("alog", "ssd_a_log"), ("sink", "swa_sinks")):
            off, n = BC_OFF["%s%d" % (nm, j)]
            bc[off:off + n] = np.asarray(inp[key][j], np.float32)
    return pp, bc


def load_weight(c, wres, src2d, n, qn="pool", piece=1408, order=None):
    s = c.s
    wres.pieces = []
    v = src2d.rearrange("(k p) n -> p k n", p=128)
    if order is None:
        order = []
        lo = 0
        while lo < n:
            order.append((lo, min(n, lo + piece)))
            lo += piece
    for lo, hi in order:
        b = Buf(None, "wp")
        b.r = dict(wres.rd.r)
        s.dma(qn, lambda e, lo=lo, hi=hi: e.dma_start(out=wres.h[:, :, lo:hi], in_=v[:, :, lo:hi]), writes=[b])
        wres.pieces.append((lo, hi, b))


def wres_new(c, name, kc, n):
    w = WRes(c.pst.enter_context(c.nc.sbuf_tensor("p%d_%s" % (c.ip, name), [128, kc, n], BF16)), kc, n)
    w.rd = Buf(None, name + "_rd")
    return w


def mm(c, out_ap, out_buf, w, lo, hi, rhs_aps, rhs_bufs, first=True, last=True):
    s = c.s
    wb = w.bufs(lo, hi)
    kc = len(rhs_aps)
    for k in range(kc):
        s.op("pe", lambda e, k=k: e.matmul(out_ap, lhsT=w.h[:, k, lo:hi], rhs=rhs_aps[k],
                                            start=(first and k == 0), stop=(last and k == kc - 1)),
             reads=wb + [w.rd] + list(rhs_bufs), writes=[out_buf])


def alloc_x(c, n):
    c.x32 = [c.psb("x32_%d" % i, [128, 8, TT]) for i in range(n)]
    for t_ in c.x32:
        t_.ch = [Buf(None, t_.name + "_c%d" % k) for k in range(8)]
    c.xb = c.psb("xb", [128, 8, TT], BF16)


def load_x(c, src, ti, NT):
    s = c.s
    xt = c.x32[ti % len(c.x32)]
    s.dma("sp", lambda e: e.dma_start(out=xt[:], in_=src[:, :, ti * TT:(ti + 1) * TT].rearrange("k p t -> p k t")),
          reads=[c.xdram[ti]], writes=xt.ch)
    for k in range(8):
        s.op("pool", lambda e, k=k: e.tensor_copy(out=c.xb[:, k, :], in_=xt[:, k, :]), reads=[xt.ch[k]], writes=[c.xb])
    return xt


class LNState:
    pass


def ln_begin(c):
    st = LNState()
    st.n = 0
    return st


def ln_chunk(c, st, xt, ci, m_ps, m_buf, sum_b, sq_b):
    s = c.s
    s.op("dve", lambda e: e.scalar_tensor_tensor(out=xt[:, ci, :], in0=xt[:, ci, :], scalar=ALPHA, in1=m_ps,
                                                  op0=ALU.mult, op1=ALU.add), reads=[xt.ch[ci], m_buf], writes=[xt.ch[ci]])
    lw = c.lnw[ci % 2]
    s.op("act", lambda e: e.activation(out=lw[:, 0, :], in_=xt[:, ci, :], func=AF.Copy), reads=[xt.ch[ci]], writes=[lw])
    s.op("act", lambda e: e.activation(out=lw[:, 1, :], in_=xt[:, ci, :], func=AF.Square), reads=[xt.ch[ci]], writes=[lw])
    s.op("pe", lambda e: e.matmul(sum_b[:], lhsT=c.ones_ln[:], rhs=lw[:, 0, :], start=(ci == 0), stop=(ci == 7)),
         reads=[c.ones_ln, lw], writes=[sum_b])
    s.op("pe", lambda e: e.matmul(sq_b[:], lhsT=c.ones_ln[:], rhs=lw[:, 1, :], start=(ci == 0), stop=(ci == 7)),
         reads=[c.ones_ln, lw], writes=[sq_b])


def ln_finish(c, xt, sum_b, sq_b, layer, which, dst, ti):
    s = c.s
    a, b = c.lns
    s.op("act", lambda e: e.activation(out=a[:], in_=sum_b[:], func=AF.Square), reads=[sum_b], writes=[a])
    s.op("dve", lambda e: e.tensor_tensor(out=a[:], in0=sq_b[:], in1=a[:], op=ALU.subtract), reads=[sq_b, a], writes=[a])
    s.op("act", lambda e: e.activation(out=a[:], in_=a[:], func=AF.Sqrt, bias=c.eps_ln[:], scale=1.0), reads=[a, c.eps_ln], writes=[a])
    s.op("dve", lambda e: e.reciprocal(out=b[:], in_=a[:]), reads=[a], writes=[b])
    for ci in range(8):
        s.op("dve", lambda e, ci=ci: e.tensor_tensor(out=xt[:, ci, :], in0=xt[:, ci, :], in1=sum_b[:], op=ALU.subtract),
             reads=[xt.ch[ci], sum_b], writes=[xt.ch[ci]])
        s.op("pool", lambda e, ci=ci: e.tensor_tensor(out=xt[:, ci, :], in0=xt[:, ci, :], in1=b[:], op=ALU.mult),
             reads=[xt.ch[ci], b], writes=[xt.ch[ci]])
        g = ppv(c, "ln_g%d" % layer, which * 8 + ci)
        bb = ppv(c, "ln_b%d" % layer, which * 8 + ci)
        s.op("act", lambda e, ci=ci, g=g, bb=bb: e.activation(out=xt[:, ci, :], in_=xt[:, ci, :], func=AF.Identity,
                                                           scale=g, bias=bb), reads=[xt.ch[ci], c.pp], writes=[xt.ch[ci]])
    return s.dma("sp", lambda e: e.dma_start(out=dst[:, :, ti * TT:(ti + 1) * TT].rearrange("k p t -> p k t"), in_=xt[:]),
                 reads=xt.ch, writes=[c.xdram[ti]])


def conv_chunk(c, h_ps, h_buf, yc, taps, wname, bname, ci, nch, halo, ti, t1=None):
    s = c.s
    K = taps
    woff = PP_OFF[wname][0]
    boff = PP_OFF[bname][0]
    halo_r = halo[(ti + 1) % 2]
    halo_w = halo[ti % 2]

    def wk(k):
        return c.pp[:, woff + k * nch + ci: woff + k * nch + ci + 1]
    bias = c.pp[:, boff + ci: boff + ci + 1]
    s.op("act", lambda e: e.activation(out=yc[:], in_=h_ps, func=AF.Identity, scale=wk(K - 1), bias=bias),
         reads=[h_buf, c.pp], writes=[yc])
    for d in range(K - 1, 0, -1):
        w = wk(K - 1 - d)
        if d == 1 and t1 is not None:
            s.op("act", lambda e, w=w: e.activation(out=t1[:], in_=h_ps, func=AF.Identity, scale=w),
                 reads=[h_buf, c.pp], writes=[t1])
        else:
            s.op("dve", lambda e, d=d, w=w: e.scalar_tensor_tensor(out=yc[:, d:TT], in0=h_ps[:, 0:TT - d], scalar=w,
                                                                in1=yc[:, d:TT], op0=ALU.mult, op1=ALU.add),
                 reads=[h_buf, yc, c.pp], writes=[yc])
    s.op("act", lambda e: e.activation(out=halo_w[:, ci, :], in_=h_ps[:, TT - (K - 1):TT], func=AF.Copy),
         reads=[h_buf], writes=[halo_w])
    if t1 is not None:
        s.op("pool", lambda e: e.tensor_tensor(out=yc[:, 1:TT], in0=yc[:, 1:TT], in1=t1[:, 0:TT - 1], op=ALU.add),
             reads=[yc, t1], writes=[yc])
    for d in range(1, K):
        w = wk(K - 1 - d)
        s.op("dve", lambda e, d=d, w=w: e.scalar_tensor_tensor(out=yc[:, 0:d], in0=halo_r[:, ci, K - 1 - d:K - 1], scalar=w,
                                                            in1=yc[:, 0:d], op0=ALU.mult, op1=ALU.add),
             reads=[halo_r, yc, c.pp], writes=[yc])


def ffn_phase(c, layer, src, dst, NT):
    s = c.s
    c.w_up = wres_new(c, "w_up", 8, 2 * FFN)
    alloc_x(c, 2)
    c.wd = [c.psb("wd%d" % i, [128, 22, 128], BF16) for i in range(3)]
    c.act = c.psb("ffn_act", [128, 22, TT], BF16)
    c.yc = [c.psb("yc%d" % i, [128, TT]) for i in range(6)]
    c.fhalo = [c.psb("fhalo%d" % i, [128, 44, 2]) for i in range(2)]
    c.t1 = [c.psb("t1_%d" % i, [128, TT]) for i in range(4)]
    order = []
    for jb in range(0, 22, 4):
        je = min(22, jb + 4)
        order.append((jb * 128, je * 128))
        order.append((FFN + jb * 128, FFN + je * 128))
    load_weight(c, c.w_up, c.ffn_w_up[layer], 2 * FFN, order=order)
    for h_ in c.fhalo:
        s.op("dve", lambda e, h_=h_: e.memset(h_[:], 0.0), writes=[h_])
    wdv = c.ffn_w_down[layer].rearrange("(j p) n -> p j n", p=128)
    wdb = [Buf(None, "wdb%d" % m) for m in range(8)]
    for m in range(8):
        wd = c.wd[m % 3]
        s.dma("pool", lambda e, wd=wd, m=m: e.dma_start(out=wd[:], in_=wdv[:, :, m * 128:(m + 1) * 128]), writes=[wd])
        s.dma("sp", lambda e, wd=wd, m=m: e.dma_start(out=c.wdb_d[m], in_=wd[:].rearrange("p j n -> p (j n)")), reads=[wd], writes=[wdb[m]])
    toks = []
    bk = c.banks
    xt_next = load_x(c, src, 0, NT)
    wi = 0
    pending = None
    for ti in range(NT):
        xt = xt_next
        rhs = [c.xb[:, k, :] for k in range(8)]
        for j in range(22):
            pg = bk[j % 3] if 'b2' not in DBG else bk[j % 2]
            pu = bk[3 + j % 3] if 'b2' not in DBG else bk[2 + j % 2]
            mm(c, pg[:], pg, c.w_up, j * 128, (j + 1) * 128, rhs, [c.xb])
            mm(c, pu[:], pu, c.w_up, (22 + j) * 128, (23 + j) * 128, rhs, [c.xb])
            yg = c.yc[j % 3]
            yu = c.yc[3 + j % 3]
            conv_chunk(c, pg[:], pg, yg, 3, "fcw%d" % layer, "fcb%d" % layer, j, 44, c.fhalo, ti, t1=(c.t1[j % 2] if 't1' in DBG else None))
            conv_chunk(c, pu[:], pu, yu, 3, "fcw%d" % layer, "fcb%d" % layer, 22 + j, 44, c.fhalo, ti, t1=(c.t1[2 + j % 2] if 't1' in DBG else None))
            s.op("act", lambda e, yg=yg: e.activation(out=yg[:], in_=yg[:], func=AF.Silu), reads=[yg], writes=[yg])
            s.op("pool", lambda e, yg=yg, yu=yu, j=j: e.tensor_tensor(out=c.act[:, j, :], in0=yg[:], in1=yu[:], op=ALU.mult),
                 reads=[yg, yu], writes=[c.act])
            if j == 3 and pending is not None:
                toks.append(pending())
                pending = None
        if ti + 1 < NT:
            xt_next = load_x(c, src, ti + 1, NT)
        lst = ln_begin(c)
        for m in range(8):
            wd = c.wd[wi % 3]
            wi += 1
            s.dma("sp", lambda e, wd=wd, m=m: e.dma_start(out=wd[:].rearrange("p j n -> p (j n)"), in_=c.wdb_d[m]), reads=[wdb[m]], writes=[wd])
            pd = bk[2 + 3 * (m % 2)] if 'b2' not in DBG else bk[4 + m % 2]
            for j in range(22):
                s.op("pe", lambda e, wd=wd, pd=pd, j=j: e.matmul(pd[:], lhsT=wd[:, j, :], rhs=c.act[:, j, :],
                                                                 start=(j == 0), stop=(j == 21)),
                     reads=[wd, c.act], writes=[pd])
            ln_chunk(c, lst, xt, m, pd[:], pd, bk[6], bk[7])
        pending = (lambda xt=xt, ti=ti: ln_finish(c, xt, bk[6], bk[7], layer, 1, dst, ti))
    toks.append(pending())
    return toks


def ab_phase(c, layer, src, dst, NT):
    s = c.s
    j = layer // 2
    bk = c.banks
    psb = c.psb
    w_in = wres_new(c, "ab_w_in", 8, 3336)
    w_out = wres_new(c, "ab_w_out", 8, 1024)
    load_weight(c, w_in, c.ab_w_in[j], 3336, piece=834)
    load_weight(c, w_out, c.ab_w_out[j], 1024, piece=512)
    alloc_x(c, 2)
    cst = psb("cst", [128, 512])
    sel = psb("sel", [8, 8, 128])
    s.dma("sp", lambda e: e.dma_start(out=cst[:], in_=c.cst_in), writes=[cst])
    s.dma("sp", lambda e: e.dma_start(out=sel[:], in_=c.sel_in), writes=[sel])
    triU, onesf, ident, negm = cst[:, 0:128], cst[:, 128:256], cst[:, 256:384], cst[:, 384:512]
    negm4 = psb("negm4", [128, 4, 128])
    for q_ in range(4):
        s.op("dve", lambda e, q_=q_: e.tensor_copy(out=negm4[:, q_, :], in_=negm), reads=[cst], writes=[negm4])
    ones256 = psb("ones256", [128, 128], BF16)
    s.op("dve", lambda e: e.memset(ones256[:], 1.0 / 256), writes=[ones256])
    ones128 = psb("ones128", [128, 128], BF16)
    s.op("dve", lambda e: e.memset(ones128[:], 1.0 / 128), writes=[ones128])
    onesb = psb("onesb", [128, 64])
    s.op("dve", lambda e: e.memset(onesb[:], 1.0), writes=[onesb])
    eps_r = psb("eps_r", [128, 1])
    s.op("dve", lambda e: e.memset(eps_r[:], RMS_EPS), writes=[eps_r])
    dtb4 = psb("dtb4", [128, 4, 8])
    a4 = psb("a4", [128, 4, 8])
    o_dtb = BC_OFF["dtb%d" % j][0]
    o_al = BC_OFF["alog%d" % j][0]
    for q_ in range(4):
        s.op("dve", lambda e, q_=q_: e.tensor_copy(out=dtb4[:, q_, :], in_=c.bcp[:, o_dtb:o_dtb + 8]), reads=[c.bcp], writes=[dtb4])
        s.op("act", lambda e, q_=q_: e.activation(out=a4[:, q_, :], in_=c.bcp[:, o_al:o_al + 8], func=AF.Exp), reads=[c.bcp], writes=[a4])
    s.op("dve", lambda e: e.tensor_scalar(out=a4[:], in0=a4[:], scalar1=-1.0, scalar2=None, op0=ALU.mult), reads=[a4], writes=[a4])
    lb = psb("lb", [128, 4])
    oml = psb("oml", [128, 4])
    if j == 0:
        s.op("dve", lambda e: e.memset(lb[:], 0.0), writes=[lb])
    else:
        s.op("dve", lambda e: e.tensor_tensor(out=lb[:], in0=ppv(c, "hgl1"), in1=ppv(c, "hgl0"), op=ALU.subtract), reads=[c.pp], writes=[lb])
        s.op("act", lambda e: e.activation(out=lb[:], in_=lb[:], func=AF.Sigmoid), reads=[lb], writes=[lb])
    s.op("dve", lambda e: e.tensor_scalar(out=oml[:], in0=lb[:], scalar1=-1.0, scalar2=1.0, op0=ALU.mult, op1=ALU.add), reads=[lb], writes=[oml])
    shalo = [psb("shalo%d" % i, [128, 6, 3]) for i in range(2)]
    for h_ in shalo:
        s.op("dve", lambda e, h_=h_: e.memset(h_[:], 0.0), writes=[h_])
    S32 = psb("S32", [128, 256])
    s.op("dve", lambda e: e.memset(S32[:], 0.0), writes=[S32])
    Spad = psb("Spad", [128, 8, 64], BF16)
    s.op("dve", lambda e: e.memset(Spad[:], 0.0), writes=[Spad])
    HS32 = [psb("HS32_%d" % h, [128, 128]) for h in range(4)]
    HSb = [psb("HSb_%d" % h, [128, 128], BF16) for h in range(4)]
    for h in range(4):
        s.op("dve", lambda e, h=h: e.memset(HS32[h][:], 0.0), writes=[HS32[h]])
        s.op("pool", lambda e, h=h: e.memset(HSb[h][:], 0.0), writes=[HSb[h]])
    BTpad = [psb("BTpad%d" % g, [128, TT], BF16) for g in range(2)]
    for g in range(2):
        s.op("pool", lambda e, g=g: e.memset(BTpad[g][:], 0.0), writes=[BTpad[g]])
    vt = [psb("vt%d" % i, [128, 512], BF16) for i in range(8)]
    for i in range(8):
        s.op("pool", lambda e, i=i: e.memset(vt[i][:], 0.0), writes=[vt[i]])
    attb = [psb("attb%d" % i, [128, 64], BF16) for i in range(2)]
    ktk = [psb("ktk%d" % i, [128, 128], BF16) for i in range(2)]
    for i in range(2):
        s.op("pool", lambda e, i=i: e.memset(attb[i][:], 0.0), writes=[attb[i]])
        s.op("pool", lambda e, i=i: e.memset(ktk[i][:], 0.0), writes=[ktk[i]])
    zs = psb("zs", [128, 4, TT], BF16)
    xs32 = psb("xs32", [128, 4, TT])
    B32 = psb("B32", [128, TT])
    CTb = psb("CTb", [128, TT], BF16)
    yc = psb("yc", [128, TT])
    dt = psb("dt", [128, 32])
    adt = psb("adt", [128, 32])
    cstok = psb("cstok", [128, 32])
    ncstok = psb("ncstok", [128, 32])
    csT = psb("csT", [8, 512])
    ncsT = psb("ncsT", [8, 512])
    ecsT = psb("ecsT", [8, 512])
    dstate = psb("dstate", [128, 32])
    dec = psb("dec", [128, 32])
    dec2 = psb("dec2", [128, 4, 4])
    Dm = [psb("Dm%d" % i, [128, 512]) for i in range(2)]
    scT = psb("scT", [128, 8, 128], BF16)
    CTs = psb("CTs", [128, 8, 128], BF16)
    xc = psb("xc_", [128, 512], BF16)
    xcd = psb("xcd", [128, 512], BF16)
    Btok = psb("Btok", [128, 128], BF16)
    ysb = psb("ysb", [128, 4, TT])
    yv2 = [psb("yv%d" % i, [128, TT]) for i in range(2)]
    yv = [yv2[0], yv2[1], yv2[0], yv2[1]]
    sqb = [psb("sqb%d" % i, [128, TT], BF16) for i in range(2)]
    ymix = psb("ymix", [128, 8, TT], BF16)
    q32 = AliasBuf(ysb, ysb[:, 0, :], "q32")
    k32 = AliasBuf(ysb, ysb[:, 1, :], "k32")
    lg = AliasBuf(ysb, ysb[:, 2, :], "lg")
    rr = yc
    bc = AliasBuf(ysb, ysb[:, 3, :], "bc")
    e1 = yc
    ex = B32
    qp = psb("qp", [128, TT], BF16)
    kp = psb("kp", [128, TT], BF16)
    qpp = psb("qpp", [128, TT], BF16)
    kppp = Dm[0]
    hdec = psb("hdec", [128, 8])
    hgs = Dm[1]
    o32 = q32
    d_off = PP_OFF["sd%d" % j][0]
    snw_off = PP_OFF["snw%d" % j][0]
    hnw = ppv(c, "hnw%d" % j)
    toks = []
    xt_next = load_x(c, src, 0, NT)
    pending = None
    for ti in range(NT):
        xt = xt_next
        rhs = [c.xb[:, k, :] for k in range(8)]
        for ch in range(4):
            ps = bk[ch % 2]
            mm(c, ps[:], ps, w_in, ch * 128, (ch + 1) * 128, rhs, [c.xb])
            s.op("act", lambda e, ps=ps, ch=ch: e.activation(out=zs[:, ch, :], in_=ps[:], func=AF.Silu), reads=[ps], writes=[zs])
        if pending is not None:
            toks.append(pending())
            pending = None
        for ch in range(6):
            ps = bk[ch % 2]
            mm(c, ps[:], ps, w_in, 512 + ch * 128, 512 + (ch + 1) * 128, rhs, [c.xb])
            conv_chunk(c, ps[:], ps, yc, 4, "scw%d" % j, "scb%d" % j, ch, 6, shalo, ti)
            if ch < 4:
                s.op("act", lambda e, ch=ch: e.activation(out=xs32[:, ch, :], in_=yc[:], func=AF.Silu), reads=[yc], writes=[xs32])
            elif ch == 4:
                s.op("act", lambda e: e.activation(out=B32[:], in_=yc[:], func=AF.Silu), reads=[yc], writes=[B32])
                s.op("pool", lambda e: e.tensor_copy(out=BTpad[0][0:64, :], in_=B32[0:64, :]), reads=[B32], writes=[BTpad[0]])
                s.op("pool", lambda e: e.tensor_copy(out=BTpad[1][64:128, :], in_=B32[64:128, :]), reads=[B32], writes=[BTpad[1]])
            else:
                s.op("act", lambda e: e.activation(out=CTb[:], in_=yc[:], func=AF.Silu), reads=[yc], writes=[CTb])
        ps = bk[0]
        wb = w_in.bufs(1280, 1288)
        for cc in range(4):
            for k in range(8):
                s.op("pe", lambda e, cc=cc, k=k, ps=ps: e.matmul(ps[:, cc * 8:(cc + 1) * 8], lhsT=c.xb[:, k, cc * 128:(cc + 1) * 128],
                                                               rhs=w_in.h[:, k, 1280:1288], start=(k == 0), stop=(k == 7)),
                     reads=wb + [w_in.rd, c.xb], writes=[ps])
        s.op("dve", lambda e, ps=ps: e.tensor_tensor(out=dt[:], in0=ps[:, 0:32], in1=dtb4[:].rearrange("p a b -> p (a b)"), op=ALU.add),
             reads=[ps, dtb4], writes=[dt])
        s.op("act", lambda e: e.activation(out=dt[:], in_=dt[:], func=AF.Exp), reads=[dt], writes=[dt])
        s.op("act", lambda e: e.activation(out=dt[:], in_=dt[:], func=AF.Ln, bias=1.0), reads=[dt], writes=[dt])
        s.op("dve", lambda e: e.tensor_tensor(out=adt[:], in0=dt[:], in1=a4[:].rearrange("p a b -> p (a b)"), op=ALU.mult), reads=[dt, a4], writes=[adt])
        ps = bk[1]
        s.op("pe", lambda e, ps=ps: e.matmul(ps[:, 0:32], lhsT=triU, rhs=adt[:], start=True, stop=True), reads=[cst, adt], writes=[ps])
        s.op("pe", lambda e, ps=ps: e.matmul(ps[:, 32:64], lhsT=onesf, rhs=adt[:], start=True, stop=True), reads=[cst, adt], writes=[ps])
        s.op("act", lambda e, ps=ps: e.activation(out=cstok[:], in_=ps[:, 0:32], func=AF.Copy), reads=[ps], writes=[cstok])
        s.op("act", lambda e, ps=ps: e.activation(out=ncstok[:], in_=ps[:, 0:32], func=AF.Copy, scale=-1.0), reads=[ps], writes=[ncstok])
        s.op("dve", lambda e, ps=ps: e.tensor_tensor(out=dstate[:], in0=ps[:, 32:64], in1=cstok[:], op=ALU.subtract), reads=[ps, cstok], writes=[dstate])
        s.op("act", lambda e: e.activation(out=dstate[:], in_=dstate[:], func=AF.Exp), reads=[dstate], writes=[dstate])
        s.op("act", lambda e, ps=ps: e.activation(out=dec[:], in_=ps[:, 32:64], func=AF.Exp), reads=[ps], writes=[dec])
        dv = dec[:].rearrange("p (a b) -> p a b", a=4)
        s.op("dve", lambda e: e.tensor_copy(out=dec2[0:64, :, :], in_=dv[0:64, :, 0:4]), reads=[dec], writes=[dec2])
        s.op("dve", lambda e: e.tensor_copy(out=dec2[64:128, :, :], in_=dv[64:128, :, 4:8]), reads=[dec], writes=[dec2])
        ps = bk[2]
        for cc in range(4):
            s.op("pe", lambda e, ps=ps, cc=cc: e.transpose(ps[0:8, cc * 128:(cc + 1) * 128], cstok[:, cc * 8:(cc + 1) * 8], ident), reads=[cstok, cst], writes=[ps])
        s.op("act", lambda e, ps=ps: e.activation(out=csT[:], in_=ps[0:8, :], func=AF.Copy), reads=[ps], writes=[csT])
        s.op("act", lambda e, ps=ps: e.activation(out=ncsT[:], in_=ps[0:8, :], func=AF.Copy, scale=-1.0), reads=[ps], writes=[ncsT])
        s.op("act", lambda e, ps=ps: e.activation(out=ecsT[:], in_=ps[0:8, :], func=AF.Exp), reads=[ps], writes=[ecsT])
        for cc in range(4):
            cols = slice(cc * 128, (cc + 1) * 128)
            for half in range(2):
                dps = bk[2 + half]
                eps_ = bk[4 + half]
                for hq_ in range(4):
                    i = half * 4 + hq_
                    s.op("pe", lambda e, dps=dps, hq_=hq_, i=i, cols=cols: e.matmul(dps[:, hq_ * 128:(hq_ + 1) * 128], lhsT=sel[:, i, :], rhs=csT[:, cols],
                                                                        start=True, stop=False), reads=[sel, csT], writes=[dps])
                    s.op("pe", lambda e, dps=dps, hq_=hq_, i=i, cols=cols: e.matmul(dps[:, hq_ * 128:(hq_ + 1) * 128], lhsT=ncsT[:, cols], rhs=sel[:, i, :],
                                                                        start=False, stop=True), reads=[sel, ncsT], writes=[dps])
                    s.op("pe", lambda e, eps_=eps_, hq_=hq_, i=i, cols=cols: e.matmul(eps_[:, hq_ * 128:(hq_ + 1) * 128], lhsT=sel[:, i, :], rhs=ecsT[:, cols],
                                                                          start=True, stop=True), reads=[sel, ecsT], writes=[eps_])
                dm = Dm[half]
                s.op("dve", lambda e, dps=dps, dm=dm: e.tensor_tensor(out=dm[:], in0=dps[:], in1=negm4[:].rearrange("p a b -> p (a b)"), op=ALU.add),
                     reads=[dps, negm4], writes=[dm])
                s.op("act", lambda e, dm=dm: e.activation(out=dm[:], in_=dm[:], func=AF.Exp), reads=[dm], writes=[dm])
                s.op("dve", lambda e, eps_=eps_, half=half, cols=cols: e.tensor_tensor(
                    out=CTs[:, half * 4:(half + 1) * 4, :], in0=eps_[:].rearrange("p (a b) -> p a b", a=4),
                    in1=CTb[:, cols].unsqueeze(1).to_broadcast([128, 4, 128]), op=ALU.mult), reads=[eps_, CTb], writes=[CTs])
            for g in range(2):
                gps = bk[6 + g]
                for r_ in range(4):
                    s.op("pe", lambda e, gps=gps, r_=r_, g=g, cols=cols: e.matmul(gps[:, r_ * 128:(r_ + 1) * 128], lhsT=BTpad[g][:, cols], rhs=CTb[:, cols],
                                                                               start=True, stop=True), reads=[BTpad[g], CTb], writes=[gps])
                s.op("dve", lambda e, gps=gps, g=g: e.tensor_tensor(out=scT[:, g * 4:(g + 1) * 4, :], in0=gps[:].rearrange("p (a b) -> p a b", a=4),
                                                                   in1=Dm[g][:].rearrange("p (a b) -> p a b", a=4), op=ALU.mult),
                     reads=[gps, Dm[g]], writes=[scT])
            xps = bk[0]
            for ch in range(4):
                s.op("pe", lambda e, xps=xps, ch=ch, cols=cols: e.transpose(xps[:, ch * 128:(ch + 1) * 128], xs32[:, ch, cols], ident),
                     reads=[xs32, cst], writes=[xps])
            s.op("dve", lambda e, xps=xps, cc=cc: e.tensor_tensor(out=xc[:].rearrange("p (h q) -> p h q", h=8), in0=xps[:].rearrange("p (h q) -> p h q", h=8),
                                                               in1=dt[:, cc * 8:(cc + 1) * 8].unsqueeze(2).to_broadcast([128, 8, 64]), op=ALU.mult),
                 reads=[xps, dt], writes=[xc])
            s.op("pool", lambda e, cc=cc: e.tensor_tensor(out=xcd[:].rearrange("p (h q) -> p h q", h=8), in0=xc[:].rearrange("p (h q) -> p h q", h=8),
                                                        in1=dstate[:, cc * 8:(cc + 1) * 8].unsqueeze(2).to_broadcast([128, 8, 64]), op=ALU.mult),
                 reads=[xc, dstate], writes=[xcd])
            bps = bk[1]
            s.op("pe", lambda e, bps=bps, cols=cols: e.transpose(bps[:, 0:128], B32[:, cols], ident), reads=[B32, cst], writes=[bps])
            s.op("act", lambda e, bps=bps: e.activation(out=Btok[:], in_=bps[:, 0:128], func=AF.Copy), reads=[bps], writes=[Btok])
            yps = bk[6]
            for h in range(8):
                oap = yps[(h % 2) * 64:(h % 2) * 64 + 64, (h // 2) * 128:(h // 2) * 128 + 128]
                s.op("pe", lambda e, oap=oap, h=h: e.matmul(oap, lhsT=xc[:, h * 64:(h + 1) * 64], rhs=scT[:, h, :], start=True, stop=False),
                     reads=[xc, scT], writes=[yps])
                s.op("pe", lambda e, oap=oap, h=h: e.matmul(oap, lhsT=Spad[:, h, :], rhs=CTs[:, h, :], start=False, stop=True),
                     reads=[Spad, CTs], writes=[yps])
            s.op("act", lambda e, yps=yps, cols=cols: e.activation(out=ysb[:, :, cols], in_=yps[:].rearrange("p (a b) -> p a b", a=4), func=AF.Copy),
                 reads=[yps], writes=[ysb])
            sps = bk[7]
            for g in range(2):
                s.op("pe", lambda e, sps=sps, g=g: e.matmul(sps[g * 64:(g + 1) * 64, 0:256], lhsT=Btok[:, g * 64:(g + 1) * 64], rhs=xcd[:, g * 256:(g + 1) * 256],
                                                           start=True, stop=True), reads=[Btok, xcd], writes=[sps])
            s.op("dve", lambda e, cc=cc: e.tensor_tensor(out=S32[:].rearrange("p (a b) -> p a b", a=4), in0=S32[:].rearrange("p (a b) -> p a b", a=4),
                                                       in1=dec2[:, cc, :].unsqueeze(2).to_broadcast([128, 4, 64]), op=ALU.mult), reads=[S32, dec2], writes=[S32])
            s.op("dve", lambda e, sps=sps: e.tensor_tensor(out=S32[:], in0=S32[:], in1=sps[:, 0:256], op=ALU.add), reads=[S32, sps], writes=[S32])
            s.op("act", lambda e: e.activation(out=Spad[0:64, 0:4, :], in_=S32[0:64, :].rearrange("p (a b) -> p a b", a=4), func=AF.Copy), reads=[S32], writes=[Spad])
            s.op("act", lambda e: e.activation(out=Spad[64:128, 4:8, :], in_=S32[64:128, :].rearrange("p (a b) -> p a b", a=4), func=AF.Copy), reads=[S32], writes=[Spad])
        for g in range(2):
            mps = bk[4 + g]
            for q_ in range(2):
                ch = 2 * g + q_
                y_ = yv[ch]
                s.op("dve", lambda e, ch=ch, y_=y_: e.scalar_tensor_tensor(out=y_[:], in0=xs32[:, ch, :], scalar=c.pp[:, d_off + ch:d_off + ch + 1],
                                                                          in1=ysb[:, ch, :], op0=ALU.mult, op1=ALU.add), reads=[xs32, ysb, c.pp], writes=[y_])
                s.op("pool", lambda e, ch=ch, y_=y_: e.tensor_tensor(out=y_[:], in0=y_[:], in1=zs[:, ch, :], op=ALU.mult), reads=[y_, zs], writes=[y_])
                sq = sqb[q_]
                s.op("act", lambda e, y_=y_, sq=sq: e.activation(out=sq[:], in_=y_[:], func=AF.Square), reads=[y_], writes=[sq])
                s.op("pe", lambda e, mps=mps, sq=sq, q_=q_: e.matmul(mps[:], lhsT=ones256[:], rhs=sq[:], start=(q_ == 0), stop=(q_ == 1)),
                     reads=[ones256, sq], writes=[mps])
            s.op("act", lambda e, mps=mps: e.activation(out=rr[:], in_=mps[:], func=AF.Sqrt, bias=eps_r[:]), reads=[mps, eps_r], writes=[rr])
            s.op("dve", lambda e: e.reciprocal(out=rr[:], in_=rr[:]), reads=[rr], writes=[rr])
            for q_ in range(2):
                ch = 2 * g + q_
                y_ = yv[ch]
                s.op("dve", lambda e, ch=ch, y_=y_: e.scalar_tensor_tensor(out=ymix[:, ch, :], in0=y_[:], scalar=c.pp[:, snw_off + ch:snw_off + ch + 1],
                                                                          in1=rr[:], op0=ALU.mult, op1=ALU.mult), reads=[y_, rr, c.pp], writes=[ymix])
        for cc in range(8):
            ps = bk[cc % 2]
            wb = w_in.bufs(2312, 2824)
            for k in range(8):
                s.op("pe", lambda e, cc=cc, k=k, ps=ps: e.matmul(ps[0:64, :], lhsT=c.xb[:, k, cc * 64:(cc + 1) * 64], rhs=w_in.h[:, k, 2312:2824],
                                                               start=(k == 0), stop=(k == 7)), reads=wb + [w_in.rd, c.xb], writes=[ps])
            s.op("act", lambda e, cc=cc, ps=ps: e.activation(out=vt[cc][0:64, :], in_=ps[0:64, :], func=AF.Copy), reads=[ps], writes=[vt[cc]])
        for hd in range(4):
            ps = bk[0]
            mm(c, ps[:], ps, w_in, 1288 + hd * 128, 1288 + (hd + 1) * 128, rhs, [c.xb])
            s.op("act", lambda e, ps=ps: e.activation(out=q32[:], in_=ps[:], func=AF.Silu), reads=[ps], writes=[q32])
            ps = bk[1]
            mm(c, ps[:], ps, w_in, 1800 + hd * 128, 1800 + (hd + 1) * 128, rhs, [c.xb])
            s.op("act", lambda e, ps=ps: e.activation(out=lg[:], in_=ps[:], func=AF.Sigmoid), reads=[ps], writes=[lg])
            s.op("dve", lambda e, hd=hd: e.tensor_scalar(out=lg[:], in0=lg[:], scalar1=oml[:, hd:hd + 1], scalar2=lb[:, hd:hd + 1], op0=ALU.mult, op1=ALU.add),
                 reads=[lg, oml, lb], writes=[lg])
            s.op("dve", lambda e: e.tensor_scalar(out=k32[:], in0=lg[:], scalar1=-1.0, scalar2=1.0, op0=ALU.mult, op1=ALU.add), reads=[lg], writes=[k32])
            s.op("act", lambda e: e.activation(out=lg[:], in_=lg[:], func=AF.Ln), reads=[lg], writes=[lg])
            ps = bk[2]
            mm(c, ps[:], ps, w_in, 2824 + hd * 128, 2824 + (hd + 1) * 128, rhs, [c.xb])
            s.op("act", lambda e, ps=ps: e.activation(out=hgs[:], in_=ps[:], func=AF.Silu), reads=[ps], writes=[hgs])
            if hd == 3 and ti + 1 < NT:
                xt_next = load_x(c, src, ti + 1, NT)
            for cc in range(8):
                s.op("dve", lambda e, cc=cc: e.tensor_tensor_scan(out=bc[:, cc * 64:(cc + 1) * 64], data0=onesb[:], data1=lg[:, cc * 64:(cc + 1) * 64],
                                                                  initial=0.0, op0=ALU.mult, op1=ALU.add), reads=[onesb, lg], writes=[bc])
            bc3 = bc[:].rearrange("p (a b) -> p a b", a=8)
            s.op("dve", lambda e: e.tensor_tensor(out=e1[:].rearrange("p (a b) -> p a b", a=8), in0=bc3, in1=bc3[:, :, 31:32].to_broadcast([128, 8, 64]), op=ALU.subtract),
                 reads=[bc], writes=[e1])
            s.op("act", lambda e: e.activation(out=ex[:], in_=e1[:], func=AF.Exp), reads=[e1], writes=[ex])
            s.op("dve", lambda e: e.tensor_tensor(out=qp[:], in0=q32[:], in1=ex[:], op=ALU.mult), reads=[q32, ex], writes=[qp])
            s.op("act", lambda e: e.activation(out=ex[:], in_=e1[:], func=AF.Exp, scale=-1.0), reads=[e1], writes=[ex])
            s.op("dve", lambda e: e.tensor_tensor(out=kp[:], in0=k32[:], in1=ex[:], op=ALU.mult), reads=[k32, ex], writes=[kp])
            s.op("act", lambda e: e.activation(out=ex[:], in_=bc[:], func=AF.Exp), reads=[bc], writes=[ex])
            s.op("dve", lambda e: e.tensor_tensor(out=qpp[:], in0=q32[:], in1=ex[:], op=ALU.mult), reads=[q32, ex], writes=[qpp])
            s.op("act", lambda e: e.activation(out=hdec[:], in_=bc3[:, :, 63], func=AF.Exp), reads=[bc], writes=[hdec])
            s.op("dve", lambda e: e.tensor_tensor(out=e1[:].rearrange("p (a b) -> p a b", a=8), in0=bc3[:, :, 63:64].to_broadcast([128, 8, 64]), in1=bc3, op=ALU.subtract),
                 reads=[bc], writes=[e1])
            s.op("act", lambda e: e.activation(out=ex[:], in_=e1[:], func=AF.Exp), reads=[e1], writes=[ex])
            s.op("dve", lambda e: e.tensor_tensor(out=kppp[:], in0=k32[:], in1=ex[:], op=ALU.mult), reads=[k32, ex], writes=[kppp])
            ops_ = bk[3]
            for cc in range(8):
                cols = slice(cc * 64, (cc + 1) * 64)
                aps = bk[4 + cc % 2]
                s.op("pe", lambda e, aps=aps, cols=cols: e.matmul(aps[0:64, 0:64], lhsT=kp[:, cols], rhs=qp[:, cols], start=True, stop=True),
                     reads=[kp, qp], writes=[aps])
                ab_ = attb[cc % 2]
                s.op("dve", lambda e, aps=aps, ab_=ab_: e.tensor_tensor(out=ab_[0:64, :], in0=aps[0:64, 0:64], in1=c.masks[0:64, 0, 0, 0:64], op=ALU.mult),
                     reads=[aps, c.masks], writes=[ab_])
                s.op("pe", lambda e, ops_=ops_, cols=cols, cc=cc, hd=hd, ab_=ab_: e.matmul(ops_[:, cols], lhsT=vt[cc][:, hd * 128:(hd + 1) * 128], rhs=ab_[:],
                                                                                    start=True, stop=False), reads=[vt[cc], ab_], writes=[ops_])
                s.op("pe", lambda e, ops_=ops_, cols=cols, hd=hd: e.matmul(ops_[:, cols], lhsT=HSb[hd][:], rhs=qpp[:, cols], start=False, stop=True),
                     reads=[HSb[hd], qpp], writes=[ops_])
                tps = bk[6 + cc % 2]
                s.op("pe", lambda e, tps=tps, cols=cols: e.transpose(tps[0:64, 0:128], kppp[:, cols], ident), reads=[kppp, cst], writes=[tps])
                kk = ktk[cc % 2]
                s.op("act", lambda e, tps=tps, kk=kk: e.activation(out=kk[0:64, :], in_=tps[0:64, 0:128], func=AF.Copy), reads=[tps], writes=[kk])
                s.op("pe", lambda e, tps=tps, kk=kk, cc=cc, hd=hd: e.matmul(tps[:, 128:256], lhsT=kk[:], rhs=vt[cc][:, hd * 128:(hd + 1) * 128], start=True, stop=True),
                     reads=[kk, vt[cc]], writes=[tps])
                s.op("dve", lambda e, tps=tps, cc=cc, hd=hd: e.scalar_tensor_tensor(out=HS32[hd][:], in0=HS32[hd][:], scalar=hdec[:, cc:cc + 1], in1=tps[:, 128:256],
                                                                                 op0=ALU.mult, op1=ALU.add), reads=[HS32[hd], hdec, tps], writes=[HS32[hd]])
                s.op("act", lambda e, hd=hd: e.activation(out=HSb[hd][:], in_=HS32[hd][:], func=AF.Copy), reads=[HS32[hd]], writes=[HSb[hd]])
            s.op("act", lambda e, ops_=ops_: e.activation(out=o32[:], in_=ops_[:], func=AF.Copy), reads=[ops_], writes=[o32])
            sq = sqb[hd % 2]
            s.op("act", lambda e, ops_=ops_, sq=sq: e.activation(out=sq[:], in_=ops_[:], func=AF.Square), reads=[ops_], writes=[sq])
            mps = bk[4]
            s.op("pe", lambda e, mps=mps, sq=sq: e.matmul(mps[:], lhsT=ones128[:], rhs=sq[:], start=True, stop=True), reads=[ones128, sq], writes=[mps])
            s.op("act", lambda e, mps=mps: e.activation(out=rr[:], in_=mps[:], func=AF.Sqrt, bias=eps_r[:]), reads=[mps, eps_r], writes=[rr])
            s.op("dve", lambda e: e.reciprocal(out=rr[:], in_=rr[:]), reads=[rr], writes=[rr])
            s.op("dve", lambda e: e.scalar_tensor_tensor(out=o32[:], in0=o32[:], scalar=hnw, in1=rr[:], op0=ALU.mult, op1=ALU.mult), reads=[o32, rr, c.pp], writes=[o32])
            s.op("dve", lambda e, hd=hd: e.tensor_tensor(out=ymix[:, 4 + hd, :], in0=o32[:], in1=hgs[:], op=ALU.mult), reads=[o32, hgs], writes=[ymix])
        lst = ln_begin(c)
        yr = [ymix[:, k, :] for k in range(8)]
        for m in range(8):
            pd = bk[4 + m % 2]
            mm(c, pd[:], pd, w_out, m * 128, (m + 1) * 128, yr, [ymix])
            ln_chunk(c, lst, xt, m, pd[:], pd, bk[6], bk[7])
        pending = (lambda xt=xt, ti=ti: ln_finish(c, xt, bk[6], bk[7], layer, 0, dst, ti))
    toks.append(pending())
    return toks


DBG = set()


def cd_phase(c, layer, src, dst, NT):
    s = c.s
    j = layer // 2
    bk = c.banks
    psb = c.psb
    w_in = wres_new(c, "cd_w_in", 8, 1792)
    w_out = wres_new(c, "cd_w_out", 8, 1024)
    load_weight(c, w_in, c.cd_w_in[j], 1792, piece=896)
    load_weight(c, w_out, c.cd_w_out[j], 1024, piece=512)
    alloc_x(c, 2)
    qT = psb("qT", [128, 4, TT], BF16)
    KP = [[psb("KP%d%d" % (hh_, g_), [128, 128 + TT], BF16) for g_ in range(2)] for hh_ in range(2)]
    for hh_ in range(2):
        for g_ in range(2):
            s.op("pool", lambda e, b_=KP[hh_][g_]: e.memset(b_[:], 0.0), writes=[KP[hh_][g_]])
    vtok = psb("vtok", [128, 5, 128], BF16)
    PT = [psb("PT%d" % i, [128, 2, 512], BF16) for i in range(2)]
    Eb = [psb("Eb%d" % i, [128, 512], BF16) for i in range(2)]
    ymix = psb("ymix", [128, 8, TT], BF16)
    rden = psb("rden", [128, 256])
    sexp = psb("sexp", [128, 4])
    bd = psb("bd", [128, 2, 4, 128], BF16)
    cneg = psb("cneg", [128, 4])
    hst = psb("hst", [128, 4])
    rhalo = [psb("rhalo%d" % i, [128, 4, 3]) for i in range(2)]
    xc2 = [psb("xc%d" % i, [128, TT]) for i in range(2)]
    xcb2 = [psb("xcb%d" % i, [128, TT], BF16) for i in range(2)]
    rt2 = [[psb("rt%d_%d" % (p_, i), [128, TT]) for i in range(8)] for p_ in range(2)]
    s.op("act", lambda e: e.activation(out=sexp[:], in_=ppv(c, "sinkpp%d" % j), func=AF.Exp), reads=[c.pp], writes=[sexp])
    s.op("act", lambda e: e.activation(out=cneg[:], in_=ppv(c, "rlam%d" % j), func=AF.Exp, scale=-1.0), reads=[c.pp], writes=[cneg])
    s.op("act", lambda e: e.activation(out=cneg[:], in_=cneg[:], func=AF.Ln, bias=1.0), reads=[cneg], writes=[cneg])
    s.op("dve", lambda e: e.tensor_scalar(out=cneg[:], in0=cneg[:], scalar1=-8.0, scalar2=None, op0=ALU.mult), reads=[cneg], writes=[cneg])
    s.op("dve", lambda e: e.memset(hst[:], 0.0), writes=[hst])
    for h_ in rhalo:
        s.op("dve", lambda e, h_=h_: e.memset(h_[:], 0.0), writes=[h_])
    s.op("dve", lambda e: e.memset(bd[:], 0.0), writes=[bd])
    for w_ in range(2 if 'nobd' not in DBG else 0):
        for blk in range(8):
            o = (blk % 2) * 64
            s.dma("pool", lambda e, w_=w_, blk=blk, o=o: e.dma_start(out=bd[o:o + 64, w_, blk // 2, o:o + 64],
                                                                   in_=c.rg_w[j, w_, blk]), writes=[bd])
    ba = PP_OFF["rba%d" % j][0]
    bx = PP_OFF["rbx%d" % j][0]
    toks = []
    xt_next = load_x(c, src, 0, NT)
    pending = None
    for ti in range(NT):
        xt = xt_next
        rhs = [c.xb[:, k, :] for k in range(8)]
        for qc in range(4):
            ps = bk[qc % 2]
            mm(c, ps[:], ps, w_in, qc * 128, (qc + 1) * 128, rhs, [c.xb])
            s.op("act", lambda e, ps=ps, qc=qc: e.activation(out=qT[:, qc, :], in_=ps[:], func=AF.Copy), reads=[ps], writes=[qT])
        ps = bk[0]
        mm(c, ps[:], ps, w_in, 512, 640, rhs, [c.xb])
        s.op("act", lambda e, ps=ps: e.activation(out=KP[0][0][0:64, 128:128 + TT], in_=ps[0:64, :], func=AF.Copy), reads=[ps], writes=[KP[0][0]])
        s.op("act", lambda e, ps=ps: e.activation(out=KP[1][1][64:128, 128:128 + TT], in_=ps[64:128, :], func=AF.Copy), reads=[ps], writes=[KP[1][1]])
        ps = bk[1]
        mm(c, ps[0:64, :], ps, w_in, 576, 640, rhs, [c.xb])
        mm(c, ps[64:128, :], ps, w_in, 512, 576, rhs, [c.xb])
        s.op("act", lambda e, ps=ps: e.activation(out=KP[0][1][0:64, 128:128 + TT], in_=ps[0:64, :], func=AF.Copy), reads=[ps], writes=[KP[0][1]])
        s.op("act", lambda e, ps=ps: e.activation(out=KP[1][0][64:128, 128:128 + TT], in_=ps[64:128, :], func=AF.Copy), reads=[ps], writes=[KP[1][0]])
        ps = bk[0]
        wb = w_in.bufs(640, 768)
        for bi in range(4):
            for k in range(8):
                s.op("pe", lambda e, bi=bi, k=k, ps=ps: e.matmul(ps[:, bi * 128:(bi + 1) * 128], lhsT=c.xb[:, k, bi * 128:(bi + 1) * 128],
                                                               rhs=w_in.h[:, k, 640:768], start=(k == 0), stop=(k == 7)),
                     reads=wb + [w_in.rd, c.xb], writes=[ps])
        s.op("act", lambda e, ps=ps: e.activation(out=vtok[:, 1:5, :], in_=ps[:].rearrange("p (b v) -> p b v", b=4), func=AF.Copy),
             reads=[ps], writes=[vtok])
        if pending is not None:
            toks.append(pending())
            pending = None
        it = 0
        for bi in range(4 if 'noswa' not in DBG else 0):
            gb = ti * 4 + bi
            for g in range(2):
                kbs = [(0, 128 + bi * 128, 1 + bi)]
                if gb > 0:
                    kbs.append((1, bi * 128, bi))
                pt = PT[it % 2]
                for kbi, (which, kc0, slot) in enumerate(kbs):
                    sc = bk[2 + kbi]
                    for jh in range(4):
                        h = 4 * g + jh
                        hh = h % 2
                        qc = h // 2
                        Kx = KP[hh][g]
                        s.op("pe", lambda e, sc=sc, jh=jh, hh=hh, qc=qc, Kx=Kx, kc0=kc0, bi=bi: e.matmul(
                            sc[:, jh * 128:(jh + 1) * 128], lhsT=Kx[:, kc0:kc0 + 128],
                            rhs=qT[:, qc, bi * 128:(bi + 1) * 128], start=True, stop=True),
                            reads=[Kx, qT], writes=[sc])
                    eb = Eb[kbi]
                    s.op("act", lambda e, sc=sc, eb=eb: e.activation(out=eb[:], in_=sc[:], func=AF.Exp, scale=0.125), reads=[sc], writes=[eb])
                    s.op("dve", lambda e, eb=eb, pt=pt, kbi=kbi, which=which: e.tensor_tensor(
                        out=pt[:, kbi, :], in0=eb[:], in1=c.masks[:, which, :, :].rearrange("p j q -> p (j q)"), op=ALU.mult),
                        reads=[eb, c.masks], writes=[pt])
                ob = bk[4 + it % 2]
                nk = len(kbs)
                for den in range(2 if 'nopv' not in DBG else 0):
                    for par in range(2):
                        for kbi, (which, kc0, slot) in enumerate(kbs):
                            lhsT = (c.ones64[:, :] if den else vtok[:, slot, g * 64:(g + 1) * 64])
                            rhs_ap = pt[:, kbi, :].rearrange("p (jj par q) -> p jj par q", jj=2, par=2)[:, :, par, :]
                            out_ap = ob[par * 64:(par + 1) * 64, den * 256:(den + 1) * 256].rearrange("p (jj q) -> p jj q", jj=2)
                            s.op("pe", lambda e, lhsT=lhsT, rhs_ap=rhs_ap, out_ap=out_ap, kbi=kbi, nk=nk: e.matmul(
                                out_ap, lhsT=lhsT, rhs=rhs_ap, start=(kbi == 0), stop=(kbi == nk - 1)),
                                reads=[vtok, c.ones64, pt], writes=[ob])
                if 'nofin' in DBG:
                    it += 1
                    continue
                for jj in range(2):
                    sx = sexp[:, g * 2 + jj:g * 2 + jj + 1]
                    s.op("dve", lambda e, ob=ob, jj=jj, sx=sx: e.tensor_scalar(out=rden[:, jj * 128:(jj + 1) * 128],
                                                                             in0=ob[:, 256 + jj * 128:256 + (jj + 1) * 128],
                                                                             scalar1=sx, scalar2=None, op0=ALU.add),
                         reads=[ob, sexp], writes=[rden])
                s.op("dve", lambda e: e.reciprocal(out=rden[:], in_=rden[:]), reads=[rden], writes=[rden])
                s.op("dve", lambda e, ob=ob, g=g, bi=bi: e.tensor_tensor(
                    out=ymix[:, 2 * g:2 * g + 2, bi * 128:(bi + 1) * 128], in0=ob[:, 0:256].rearrange("p (jj q) -> p jj q", jj=2),
                    in1=rden[:].rearrange("p (jj q) -> p jj q", jj=2), op=ALU.mult), reads=[ob, rden], writes=[ymix])
                it += 1
        s.op("pool", lambda e: e.tensor_copy(out=vtok[:, 0, :], in_=vtok[:, 4, :]), reads=[vtok], writes=[vtok])
        for hh_ in range(2):
            for g_ in range(2):
                s.op("pool", lambda e, b_=KP[hh_][g_]: e.tensor_copy(out=b_[:, 0:128], in_=b_[:, TT:TT + 128]), reads=[KP[hh_][g_]], writes=[KP[hh_][g_]])
        for ch in range(4 if 'norg' not in DBG else 0):
            p_ = ch % 2
            psg, psx, psr, psi = bk[p_], bk[2 + p_], bk[4 + p_], bk[6 + p_]
            xc, xcb = xc2[p_], xcb2[p_]
            tr, ti_, ta, tm, tu, th, tg1, tg2 = rt2[p_]
            mm(c, psg[:], psg, w_in, 768 + ch * 128, 768 + (ch + 1) * 128, rhs, [c.xb])
            mm(c, psx[:], psx, w_in, 1280 + ch * 128, 1280 + (ch + 1) * 128, rhs, [c.xb])
            s.op("act", lambda e, psg=psg, tg1=tg1: e.activation(out=tg1[:], in_=psg[:], func=AF.Square), reads=[psg], writes=[tg1])
            s.op("act", lambda e, psg=psg, tg2=tg2: e.activation(out=tg2[:], in_=psg[:], func=AF.Copy), reads=[psg], writes=[tg2])
            conv_chunk(c, psx[:], psx, xc, 4, "rcw%d" % j, "rcb%d" % j, ch, 4, rhalo, ti)
            s.op("pool", lambda e, xc=xc, xcb=xcb: e.tensor_copy(out=xcb[:], in_=xc[:]), reads=[xc], writes=[xcb])
            s.op("pe", lambda e, ch=ch, psr=psr, xcb=xcb: e.matmul(psr[:], lhsT=bd[:, 0, ch, :], rhs=xcb[:], start=True, stop=True), reads=[bd, xcb], writes=[psr])
            s.op("pe", lambda e, ch=ch, psi=psi, xcb=xcb: e.matmul(psi[:], lhsT=bd[:, 1, ch, :], rhs=xcb[:], start=True, stop=True), reads=[bd, xcb], writes=[psi])
            s.op("pool", lambda e, tg1=tg1: e.tensor_scalar(out=tg1[:], in0=tg1[:], scalar1=0.044715, scalar2=1.0, op0=ALU.mult, op1=ALU.add),
                 reads=[tg1], writes=[tg1])
            s.op("pool", lambda e, tg1=tg1, tg2=tg2: e.tensor_tensor(out=tg1[:], in0=tg1[:], in1=tg2[:], op=ALU.mult), reads=[tg1, tg2], writes=[tg1])
            s.op("act", lambda e, ch=ch, psr=psr, tr=tr: e.activation(out=tr[:], in_=psr[:], func=AF.Sigmoid, bias=c.pp[:, ba + ch:ba + ch + 1]),
                 reads=[psr, c.pp], writes=[tr])
            s.op("act", lambda e, ch=ch, psi=psi, ti_=ti_: e.activation(out=ti_[:], in_=psi[:], func=AF.Sigmoid, bias=c.pp[:, bx + ch:bx + ch + 1]),
                 reads=[psi, c.pp], writes=[ti_])
            s.op("act", lambda e, tg1=tg1: e.activation(out=tg1[:], in_=tg1[:], func=AF.Sigmoid, scale=2.0 * math.sqrt(2.0 / math.pi)),
                 reads=[tg1], writes=[tg1])
            s.op("act", lambda e, ch=ch, ta=ta, tr=tr: e.activation(out=ta[:], in_=tr[:], func=AF.Exp, scale=cneg[:, ch:ch + 1]), reads=[tr, cneg], writes=[ta])
            s.op("pool", lambda e, tm=tm, ta=ta: e.tensor_tensor(out=tm[:], in0=ta[:], in1=ta[:], op=ALU.mult), reads=[ta], writes=[tm])
            s.op("act", lambda e, tm=tm: e.activation(out=tm[:], in_=tm[:], func=AF.Sqrt, scale=-1.0, bias=1.0), reads=[tm], writes=[tm])
            s.op("pool", lambda e, tu=tu, ti_=ti_, xc=xc: e.tensor_tensor(out=tu[:], in0=ti_[:], in1=xc[:], op=ALU.mult), reads=[ti_, xc], writes=[tu])
            s.op("dve", lambda e, tu=tu, tm=tm: e.tensor_tensor(out=tu[:], in0=tu[:], in1=tm[:], op=ALU.mult), reads=[tu, tm], writes=[tu])
            s.op("dve", lambda e, ch=ch, th=th, ta=ta, tu=tu: e.tensor_tensor_scan(out=th[:], data0=ta[:], data1=tu[:], initial=hst[:, ch:ch + 1],
                                                                              op0=ALU.mult, op1=ALU.add), reads=[ta, tu, hst], writes=[th])
            s.op("act", lambda e, ch=ch, th=th: e.activation(out=hst[:, ch:ch + 1], in_=th[:, TT - 1:TT], func=AF.Copy), reads=[th], writes=[hst])
            s.op("pool", lambda e, tg1=tg1, tg2=tg2: e.tensor_tensor(out=tg1[:], in0=tg1[:], in1=tg2[:], op=ALU.mult), reads=[tg1, tg2], writes=[tg1])
            s.op("dve", lambda e, ch=ch, th=th, tg1=tg1: e.tensor_tensor(out=ymix[:, 4 + ch, :], in0=th[:], in1=tg1[:], op=ALU.mult),
                 reads=[th, tg1], writes=[ymix])
        if ti + 1 < NT:
            xt_next = load_x(c, src, ti + 1, NT)
        lst = ln_begin(c)
        yr = [ymix[:, k, :] for k in range(8)]
        for m in range(8):
            pd = bk[4 + m % 2]
            mm(c, pd[:], pd, w_out, m * 128, (m + 1) * 128, yr, [ymix])
            ln_chunk(c, lst, xt, m, pd[:], pd, bk[6], bk[7])
        pending = (lambda xt=xt, ti=ti: ln_finish(c, xt, bk[6], bk[7], layer, 0, dst, ti))
    toks.append(pending())
    return toks


def make_masks():
    s_ = np.arange(128)[:, None]
    t_ = np.arange(128)[None, :]
    m = np.zeros((128, 2, 4, 128), np.float32)
    m[:, 0] = (s_ <= t_).astype(np.float32)[:, None, :]
    m[:, 1] = (s_ > t_).astype(np.float32)[:, None, :]
    return m


def make_cst():
    a = np.arange(128)
    triU = (a[:, None] <= a[None, :]).astype(np.float32)
    ones = np.ones((128, 128), np.float32)
    ident = np.eye(128, dtype=np.float32)
    negm = np.where(a[None, :] < a[:, None], NEG, 0.0).astype(np.float32)
    return np.ascontiguousarray(np.concatenate([triU, ones, ident, negm], axis=1))


def make_sel():
    m = np.zeros((8, 8, 128), np.float32)
    for i in range(8):
        m[i, i, :] = 1.0
    return m


_CACHE = {}


def run_cores(inputs, S, layer_list, phases, n_active, xT_list):
    key = (S, tuple(layer_list), tuple(phases))
    if key not in _CACHE:
        _CACHE[key] = build_program(S, layer_list, phases)
    nc = _CACHE[key]
    pp, bc = pack_params(inputs)
    rg_w = np.stack([np.asarray(inputs["rg_wa"], np.float32), np.asarray(inputs["rg_wx"], np.float32)], axis=1)
    base = {
        "ab_w_in": np.asarray(inputs["ab_w_in"], np.float32), "ab_w_out": np.asarray(inputs["ab_w_out"], np.float32),
        "cd_w_in": np.asarray(inputs["cd_w_in"], np.float32), "cd_w_out": np.asarray(inputs["cd_w_out"], np.float32),
        "ffn_w_up": np.asarray(inputs["ffn_w_up"], np.float32), "ffn_w_down": np.asarray(inputs["ffn_w_down"], np.float32),
        "pp": pp, "bcp": bc, "rg_w": np.ascontiguousarray(rg_w), "masks": make_masks(), "cst": make_cst(), "sel": make_sel(),
    }
    in_maps = []
    for i in range(n_active):
        m = dict(base)
        m["xT"] = xT_list[i]
        in_maps.append(m)
    res = run_bass_kernel_spmd(nc, in_maps, core_ids=list(range(n_active)))
    return [r["outT"] for r in res.results]


def kernel(**inputs):
    x = np.asarray(inputs["x"], np.float32)
    B, S, _ = x.shape
    xT = [np.ascontiguousarray(x[b].T).reshape(8, 128, S) for b in range(B)]
    outs = run_cores(inputs, S, [0, 1, 2, 3], ("mix", "ffn"), B, xT)
    out = np.stack([outs[b].reshape(1024, S).T for b in range(B)], axis=0)
    return np.ascontiguousarray(out).astype(np.float32)
```

```python
import math
from contextlib import ExitStack
import numpy as np
import concourse.bass as bass
import concourse.mybir as mybir
from concourse.bass_utils import run_bass_kernel_spmd

F32 = mybir.dt.float32
BF16 = mybir.dt.bfloat16
AF = mybir.ActivationFunctionType
ALU = mybir.AluOpType

D = 1024
DEPTH = 4
ALPHA = (2 * DEPTH) ** 0.25
FFN = 2816
TT = 512
LN_EPS = 1e-5
RMS_EPS = 1e-6
NEG = -30000.0


class Buf:
    def __init__(self, h=None, name=""):
        self.h = h
        self.name = name
        self.w = None
        self.r = {}

    def __getitem__(self, idx):
        return self.h[idx]


class AliasBuf(Buf):
    def __init__(self, parent, h, name=""):
        self.__dict__["parent"] = parent
        self.__dict__["h"] = h
        self.__dict__["name"] = name

    @property
    def w(self):
        return self.parent.w

    @w.setter
    def w(self, v):
        self.parent.w = v

    @property
    def r(self):
        return self.parent.r

    @r.setter
    def r(self, v):
        self.parent.r = v


class Sched:
    ENGS = ["pe", "act", "dve", "pool", "sp"]
    NDMASEM = 6

    def __init__(self, nc, stack):
        self.nc = nc
        self.q = {e: [] for e in self.ENGS}
        self.cnt = {e: 0 for e in self.ENGS}
        self.seen = {e: {} for e in self.ENGS}
        self.sems = {}
        for e in self.ENGS:
            self.sems[e] = stack.enter_context(nc.semaphore("s_" + e))
        self.dma_i = {}
        self.dma_val = {}
        for qn in ["sp", "act", "pool"]:
            self.dma_i[qn] = 0
            for k in range(self.NDMASEM):
                key = "d_%s_%d" % (qn, k)
                self.sems[key] = stack.enter_context(nc.semaphore(key))
                self.dma_val[key] = 0

    def _need(self, eng, tok, waits):
        if tok is None:
            return
        key, val = tok
        if self.seen[eng].get(key, 0) >= val:
            return
        for i, (k2, v2) in enumerate(waits):
            if k2 == key:
                if v2 < val:
                    waits[i] = (key, val)
                return
        waits.append((key, val))

    def _deps(self, eng, reads, writes, is_dma=False):
        waits = []
        for b in reads:
            if b.w is not None:
                if b.w[0] == eng and eng == "pe" and not is_dma:
                    continue
                self._need(eng, b.w, waits)
        for b in writes:
            if b.w is not None and not (b.w[0] == eng and not is_dma):
                self._need(eng, b.w, waits)
            for k, v in b.r.items():
                if k == eng and not is_dma:
                    continue
                self._need(eng, (k, v), waits)
        for k, v in waits:
            self.seen[eng][k] = v
        return waits

    def op(self, eng, fn, reads=(), writes=()):
        waits = self._deps(eng, reads, writes)
        self.cnt[eng] += 1
        tok = (eng, self.cnt[eng])
        self.q[eng].append((waits, fn, (eng, 1)))
        for b in reads:
            b.r[eng] = tok[1]
        for b in writes:
            b.w = tok
            b.r = {}
        return tok

    def dma(self, qn, fn, reads=(), writes=()):
        i = self.dma_i[qn]
        self.dma_i[qn] += 1
        key = "d_%s_%d" % (qn, i % self.NDMASEM)
        waits = self._deps(qn, reads, writes, is_dma=True)
        prev = self.dma_val[key]
        if prev > 0:
            self._need(qn, (key, prev), waits)
            self.seen[qn][key] = max(self.seen[qn].get(key, 0), prev)
        self.dma_val[key] = prev + 16
        tok = (key, prev + 16)
        self.q[qn].append((waits, fn, (key, 16)))
        for b in reads:
            b.r[key] = tok[1]
        for b in writes:
            b.w = tok
            b.r = {}
        return tok

    def wait_all(self, eng, toks):
        waits = []
        for t in toks:
            self._need(eng, t, waits)
        for k, v in waits:
            self.seen[eng][k] = v
        self.q[eng].append((waits, None, None))

    def barrier(self):
        toks = [(e, self.cnt[e]) for e in self.ENGS if self.cnt[e] > 0]
        toks += [(k, v) for k, v in self.dma_val.items() if v > 0]
        for e in self.ENGS:
            self.wait_all(e, toks)

    def emit(self, block):
        handles = {"pe": block.tensor, "act": block.scalar, "dve": block.vector,
                   "pool": block.gpsimd, "sp": block.sync}
        sems = self.sems
        for e in self.ENGS:
            def body(engine, q=self.q[e]):
                for waits, fn, inc in q:
                    for k, v in waits:
                        engine.wait_ge(sems[k], v)
                    if fn is not None:
                        fn(engine).then_inc(sems[inc[0]], inc[1])
            handles[e](body)
        self.q = {e: [] for e in self.ENGS}


class WRes:
    def __init__(self, h, kc, n):
        self.h = h
        self.kc = kc
        self.n = n
        self.pieces = []

    def bufs(self, lo, hi):
        return [b for (a, z, b) in self.pieces if a < hi and lo < z]


class Ctx:
    pass


def build_program(S, layer_list, phases=("mix", "ffn")):
    nc = bass.Bass("TRN2", target_bir_lowering=False)
    NT = S // TT
    c = Ctx()
    c.nc = nc
    c.S = S

    def din(name, shape):
        return nc.dram_tensor(name, list(shape), F32, kind="ExternalInput").ap()

    c.xT = din("xT", [8, 128, S])
    c.ab_w_in = din("ab_w_in", [2, 1024, 3336])
    c.ab_w_out = din("ab_w_out", [2, 1024, 1024])
    c.cd_w_in = din("cd_w_in", [2, 1024, 1792])
    c.cd_w_out = din("cd_w_out", [2, 1024, 1024])
    c.ffn_w_up = din("ffn_w_up", [4, 1024, 5632])
    c.ffn_w_down = din("ffn_w_down", [4, 2816, 1024])
    c.pp_in = din("pp", [128, PP_N])
    c.bc_in = din("bcp", [BC_N])
    c.rg_w = din("rg_w", [2, 2, 8, 64, 64])
    c.masks_in = din("masks", [128, 2, 4, 128])
    c.cst_in = din("cst", [128, 512])
    c.sel_in = din("sel", [8, 8, 128])
    c.outT = nc.dram_tensor("outT", [8, 128, S], F32, kind="ExternalOutput").ap()
    c.xs = nc.dram_tensor("xscratch", [8, 128, S], F32).ap()
    c.wdb_d = nc.dram_tensor("wdb_scratch", [8, 128, 22 * 128], BF16).ap()

    with ExitStack() as st:
        s = Sched(nc, st)
        c.s = s
        c.st = st

        def sb(name, shape, dt=F32):
            return Buf(st.enter_context(nc.sbuf_tensor("t_" + name, list(shape), dt)), name)
        c.sb = sb
        c.banks = [Buf(st.enter_context(nc.psum_tensor("bank%d" % i, [128, 512], F32)), "bank%d" % i)
                   for i in range(8)]
        c.pp = sb("pp", [128, PP_N])
        c.bcp = sb("bcp", [128, BC_N])
        s.dma("sp", lambda e: e.dma_start(out=c.pp[:], in_=c.pp_in), writes=[c.pp])
        s.dma("sp", lambda e: e.dma_start(out=c.bcp[:], in_=c.bc_in.partition_broadcast(128)), writes=[c.bcp])
        c.ones_ln = sb("ones_ln", [128, 128], BF16)
        s.op("dve", lambda e: e.memset(c.ones_ln[:], 1.0 / 1024), writes=[c.ones_ln])
        c.eps_ln = sb("eps_ln", [128, 1])
        s.op("dve", lambda e: e.memset(c.eps_ln[:], LN_EPS), writes=[c.eps_ln])
        c.lnw = [sb("lnw%d" % i, [128, 2, TT], BF16) for i in range(2)]
        c.lns = [sb("lns%d" % i, [128, TT]) for i in range(2)]
        c.xdram = [Buf(None, "xd%d" % t) for t in range(NT)]
        c.masks = sb("masks", [128, 2, 4, 128], BF16)
        s.dma("pool", lambda e: e.dma_start(out=c.masks[:], in_=c.masks_in), writes=[c.masks])
        c.ones64 = sb("ones64", [128, 64], BF16)
        s.op("dve", lambda e: e.memset(c.ones64[:], 1.0), writes=[c.ones64])
        c.first = True
        nph = len(layer_list) * len(phases)
        ip = 0
        for layer in layer_list:
            for ph in phases:
                ip += 1
                last = ip == nph
                src = c.xT if c.first else c.xs
                dst = c.outT if last else c.xs
                c.first = False
                with ExitStack() as pst:
                    def psb(name, shape, dt=F32, ip=ip, pst=pst):
                        return Buf(pst.enter_context(nc.sbuf_tensor("p%d_%s" % (ip, name), list(shape), dt)), name)
                    c.psb = psb
                    c.pst = pst
                    c.ip = ip
                    s.barrier()
                    if ph == "ffn":
                        toks = ffn_phase(c, layer, src, dst, NT)
                    elif layer % 2 == 0:
                        toks = ab_phase(c, layer, src, dst, NT)
                    else:
                        toks = cd_phase(c, layer, src, dst, NT)
                    if last:
                        s.wait_all("sp", toks)
                    with nc.Block() as block:
                        s.emit(block)
    return nc


PP_OFF = {}
PP_N = 0
BC_OFF = {}
BC_N = 0


def _pp_alloc(name, n):
    global PP_N
    PP_OFF[name] = (PP_N, n)
    PP_N += n


def _bc_alloc(name, n):
    global BC_N
    BC_OFF[name] = (BC_N, n)
    BC_N += n


for _l in range(4):
    _pp_alloc("ln_g%d" % _l, 16)
    _pp_alloc("ln_b%d" % _l, 16)
    _pp_alloc("fcw%d" % _l, 3 * 44)
    _pp_alloc("fcb%d" % _l, 44)
for _j in range(2):
    _pp_alloc("scw%d" % _j, 4 * 6)
    _pp_alloc("scb%d" % _j, 6)
    _pp_alloc("sd%d" % _j, 4)
    _pp_alloc("snw%d" % _j, 4)
    _pp_alloc("hgl%d" % _j, 4)
    _pp_alloc("hnw%d" % _j, 1)
    _pp_alloc("rcw%d" % _j, 4 * 4)
    _pp_alloc("rcb%d" % _j, 4)
    _pp_alloc("rba%d" % _j, 4)
    _pp_alloc("rbx%d" % _j, 4)
    _pp_alloc("rlam%d" % _j, 4)
    _pp_alloc("sinkpp%d" % _j, 4)
    _bc_alloc("dtb%d" % _j, 8)
    _bc_alloc("alog%d" % _j, 8)
    _bc_alloc("sink%d" % _j, 8)


def ppv(c, name, i=None):
    off, n = PP_OFF[name]
    if i is None:
        return c.pp[:, off:off + n]
    return c.pp[:, off + i:off + i + 1]


def pack_params(inp):
    pp = np.zeros((128, PP_N), np.float32)
    bc = np.zeros((BC_N,), np.float32)

    def put(name, arr):
        off, n = PP_OFF[name]
        a = np.asarray(arr, np.float32)
        lead = a.shape[:-1]
        nch = a.shape[-1] // 128
        a = a.reshape(lead + (nch, 128))
        a = np.moveaxis(a, -1, 0).reshape(128, -1)
        assert a.shape[1] == n, (name, a.shape, n)
        pp[:, off:off + n] = a

    for l in range(4):
        put("ln_g%d" % l, inp["ln_g"][l])
        put("ln_b%d" % l, inp["ln_b"][l])
        put("fcw%d" % l, inp["ffn_conv_w"][l])
        put("fcb%d" % l, inp["ffn_conv_b"][l])
    for j in range(2):
        put("scw%d" % j, inp["ssd_conv_w"][j])
        put("scb%d" % j, inp["ssd_conv_b"][j])
        put("sd%d" % j, np.repeat(np.asarray(inp["ssd_d"][j]), 64))
        put("snw%d" % j, inp["ssd_norm_w"][j])
        put("hgl%d" % j, inp["hg_lower"][j])
        put("hnw%d" % j, inp["hg_norm_w"][j])
        put("rcw%d" % j, inp["rg_conv_w"][j])
        put("rcb%d" % j, inp["rg_conv_b"][j])
        put("rba%d" % j, inp["rg_ba"][j])
        put("rbx%d" % j, inp["rg_bx"][j])
        put("rlam%d" % j, inp["rg_lambda"][j])
        sk = np.asarray(inp["swa_sinks"][j], np.float32)
        off, n = PP_OFF["sinkpp%d" % j]
        for g in range(2):
            for jj in range(2):
                pp[0:64, off + g * 2 + jj] = sk[4 * g + 2 * jj]
                pp[64:128, off + g * 2 + jj] = sk[4 * g + 2 * jj + 1]
        for nm, key in (("dtb", "ssd_dt_bias"), ("alog", "ssd_a_log"), ("sink", "swa_sinks")):
            off, n = BC_OFF["%s%d" % (nm, j)]
            bc[off:off + n] = np.asarray(inp[key][j], np.float32)
    return pp, bc


def load_weight(c, wres, src2d, n, qn="pool", piece=1408, order=None):
    s = c.s
    wres.pieces = []
    v = src2d.rearrange("(k p) n -> p k n", p=128)
    if order is None:
        order = []
        lo = 0
        while lo < n:
            order.append((lo, min(n, lo + piece)))
            lo += piece
    for lo, hi in order:
        b = Buf(None, "wp")
        b.r = dict(wres.rd.r)
        s.dma(qn, lambda e, lo=lo, hi=hi: e.dma_start(out=wres.h[:, :, lo:hi], in_=v[:, :, lo:hi]), writes=[b])
        wres.pieces.append((lo, hi, b))


def wres_new(c, name, kc, n):
    w = WRes(c.pst.enter_context(c.nc.sbuf_tensor("p%d_%s" % (c.ip, name), [128, kc, n], BF16)), kc, n)
    w.rd = Buf(None, name + "_rd")
    return w


def mm(c, out_ap, out_buf, w, lo, hi, rhs_aps, rhs_bufs, first=True, last=True):
    s = c.s
    wb = w.bufs(lo, hi)
    kc = len(rhs_aps)
    for k in range(kc):
        s.op("pe", lambda e, k=k: e.matmul(out_ap, lhsT=w.h[:, k, lo:hi], rhs=rhs_aps[k],
                                            start=(first and k == 0), stop=(last and k == kc - 1)),
             reads=wb + [w.rd] + list(rhs_bufs), writes=[out_buf])


def alloc_x(c, n):
    c.x32 = [c.psb("x32_%d" % i, [128, 8, TT]) for i in range(n)]
    for t_ in c.x32:
        t_.ch = [Buf(None, t_.name + "_c%d" % k) for k in range(8)]
    c.xb = c.psb("xb", [128, 8, TT], BF16)


def load_x(c, src, ti, NT):
    s = c.s
    xt = c.x32[ti % len(c.x32)]
    s.dma("sp", lambda e: e.dma_start(out=xt[:], in_=src[:, :, ti * TT:(ti + 1) * TT].rearrange("k p t -> p k t")),
          reads=[c.xdram[ti]], writes=xt.ch)
    for k in range(8):
        s.op("pool", lambda e, k=k: e.tensor_copy(out=c.xb[:, k, :], in_=xt[:, k, :]), reads=[xt.ch[k]], writes=[c.xb])
    return xt


class LNState:
    pass


def ln_begin(c):
    st = LNState()
    st.n = 0
    st.pend = None
    return st


def ln_chunk(c, st, xt, ci, m_ps, m_buf, sum_b, sq_b):
    s = c.s
    ln_flush(c, st)
    s.op("dve", lambda e: e.scalar_tensor_tensor(out=xt[:, ci, :], in0=xt[:, ci, :], scalar=ALPHA, in1=m_ps,
                                                  op0=ALU.mult, op1=ALU.add), reads=[xt.ch[ci], m_buf], writes=[xt.ch[ci]])
    lw = c.lnw[ci % 2]
    s.op("act", lambda e: e.activation(out=lw[:, 0, :], in_=xt[:, ci, :], func=AF.Copy), reads=[xt.ch[ci]], writes=[lw])
    s.op("act", lambda e: e.activation(out=lw[:, 1, :], in_=xt[:, ci, :], func=AF.Square), reads=[xt.ch[ci]], writes=[lw])
    def stats(ci=ci, lw=lw):
        s.op("pe", lambda e: e.matmul(sum_b[:], lhsT=c.ones_ln[:], rhs=lw[:, 0, :], start=(ci == 0), stop=(ci == 7)),
             reads=[c.ones_ln, lw], writes=[sum_b])
        s.op("pe", lambda e: e.matmul(sq_b[:], lhsT=c.ones_ln[:], rhs=lw[:, 1, :], start=(ci == 0), stop=(ci == 7)),
             reads=[c.ones_ln, lw], writes=[sq_b])
    st.pend = stats


def ln_flush(c, st):
    if st.pend is not None:
        st.pend()
        st.pend = None


def ln_finish(c, xt, sum_b, sq_b, layer, which, dst, ti):
    s = c.s
    a, b = c.lns
    s.op("act", lambda e: e.activation(out=a[:], in_=sum_b[:], func=AF.Square), reads=[sum_b], writes=[a])
    s.op("dve", lambda e: e.tensor_tensor(out=a[:], in0=sq_b[:], in1=a[:], op=ALU.subtract), reads=[sq_b, a], writes=[a])
    s.op("act", lambda e: e.activation(out=a[:], in_=a[:], func=AF.Sqrt, bias=c.eps_ln[:], scale=1.0), reads=[a, c.eps_ln], writes=[a])
    s.op("dve", lambda e: e.reciprocal(out=b[:], in_=a[:]), reads=[a], writes=[b])
    for ci in range(8):
        s.op("dve", lambda e, ci=ci: e.tensor_tensor(out=xt[:, ci, :], in0=xt[:, ci, :], in1=sum_b[:], op=ALU.subtract),
             reads=[xt.ch[ci], sum_b], writes=[xt.ch[ci]])
        s.op("pool", lambda e, ci=ci: e.tensor_tensor(out=xt[:, ci, :], in0=xt[:, ci, :], in1=b[:], op=ALU.mult),
             reads=[xt.ch[ci], b], writes=[xt.ch[ci]])
        g = ppv(c, "ln_g%d" % layer, which * 8 + ci)
        bb = ppv(c, "ln_b%d" % layer, which * 8 + ci)
        s.op("act", lambda e, ci=ci, g=g, bb=bb: e.activation(out=xt[:, ci, :], in_=xt[:, ci, :], func=AF.Identity,
                                                           scale=g, bias=bb), reads=[xt.ch[ci], c.pp], writes=[xt.ch[ci]])
    return s.dma("sp", lambda e: e.dma_start(out=dst[:, :, ti * TT:(ti + 1) * TT].rearrange("k p t -> p k t"), in_=xt[:]),
                 reads=xt.ch, writes=[c.xdram[ti]])


def conv_chunk(c, h_ps, h_buf, yc, taps, wname, bname, ci, nch, halo, ti, t1=None):
    s = c.s
    K = taps
    woff = PP_OFF[wname][0]
    boff = PP_OFF[bname][0]
    halo_r = halo[(ti + 1) % 2]
    halo_w = halo[ti % 2]

    def wk(k):
        return c.pp[:, woff + k * nch + ci: woff + k * nch + ci + 1]
    bias = c.pp[:, boff + ci: boff + ci + 1]
    s.op("act", lambda e: e.activation(out=yc[:], in_=h_ps, func=AF.Identity, scale=wk(K - 1), bias=bias),
         reads=[h_buf, c.pp], writes=[yc])
    for d in range(K - 1, 0, -1):
        w = wk(K - 1 - d)
        if d == 1 and t1 is not None:
            s.op("act", lambda e, w=w: e.activation(out=t1[:], in_=h_ps, func=AF.Identity, scale=w),
                 reads=[h_buf, c.pp], writes=[t1])
        else:
            s.op("dve", lambda e, d=d, w=w: e.scalar_tensor_tensor(out=yc[:, d:TT], in0=h_ps[:, 0:TT - d], scalar=w,
                                                                in1=yc[:, d:TT], op0=ALU.mult, op1=ALU.add),
                 reads=[h_buf, yc, c.pp], writes=[yc])
    s.op("act", lambda e: e.activation(out=halo_w[:, ci, :], in_=h_ps[:, TT - (K - 1):TT], func=AF.Copy),
         reads=[h_buf], writes=[halo_w])
    if t1 is not None:
        s.op("pool", lambda e: e.tensor_tensor(out=yc[:, 1:TT], in0=yc[:, 1:TT], in1=t1[:, 0:TT - 1], op=ALU.add),
             reads=[yc, t1], writes=[yc])
    for d in range(1, K):
        w = wk(K - 1 - d)
        s.op("dve", lambda e, d=d, w=w: e.scalar_tensor_tensor(out=yc[:, 0:d], in0=halo_r[:, ci, K - 1 - d:K - 1], scalar=w,
                                                            in1=yc[:, 0:d], op0=ALU.mult, op1=ALU.add),
             reads=[halo_r, yc, c.pp], writes=[yc])


def ffn_phase(c, layer, src, dst, NT):
    s = c.s
    c.w_up = wres_new(c, "w_up", 8, 2 * FFN)
    alloc_x(c, 2)
    c.wd = [c.psb("wd%d" % i, [128, 22, 128], BF16) for i in range(3)]
    c.act = c.psb("ffn_act", [128, 22, TT], BF16)
    c.yc = [c.psb("yc%d" % i, [128, TT]) for i in range(6)]
    c.fhalo = [c.psb("fhalo%d" % i, [128, 44, 2]) for i in range(2)]
    c.t1 = [c.psb("t1_%d" % i, [128, TT]) for i in range(4)]
    order = []
    for jb in range(0, 22, 4):
        je = min(22, jb + 4)
        order.append((jb * 128, je * 128))
        order.append((FFN + jb * 128, FFN + je * 128))
    load_weight(c, c.w_up, c.ffn_w_up[layer], 2 * FFN, order=order)
    for h_ in c.fhalo:
        s.op("dve", lambda e, h_=h_: e.memset(h_[:], 0.0), writes=[h_])
    wdv = c.ffn_w_down[layer].rearrange("(j p) n -> p j n", p=128)
    wdb = [Buf(None, "wdb%d" % m) for m in range(8)]
    for m in range(8):
        wd = c.wd[m % 3]
        s.dma("pool", lambda e, wd=wd, m=m: e.dma_start(out=wd[:], in_=wdv[:, :, m * 128:(m + 1) * 128]), writes=[wd])
        s.dma("sp", lambda e, wd=wd, m=m: e.dma_start(out=c.wdb_d[m], in_=wd[:].rearrange("p j n -> p (j n)")), reads=[wd], writes=[wdb[m]])
    toks = []
    bk = c.banks
    xt_next = load_x(c, src, 0, NT)
    wi = 0
    pending = None
    for ti in range(NT):
        xt = xt_next
        rhs = [c.xb[:, k, :] for k in range(8)]
        for j in range(22):
            pg = bk[j % 3] if 'b2' not in DBG else bk[j % 2]
            pu = bk[3 + j % 3] if 'b2' not in DBG else bk[2 + j % 2]
            mm(c, pg[:], pg, c.w_up, j * 128, (j + 1) * 128, rhs, [c.xb])
            mm(c, pu[:], pu, c.w_up, (22 + j) * 128, (23 + j) * 128, rhs, [c.xb])
            yg = c.yc[j % 3]
            yu = c.yc[3 + j % 3]
            conv_chunk(c, pg[:], pg, yg, 3, "fcw%d" % layer, "fcb%d" % layer, j, 44, c.fhalo, ti, t1=(c.t1[j % 2] if 't1' in DBG else None))
            conv_chunk(c, pu[:], pu, yu, 3, "fcw%d" % layer, "fcb%d" % layer, 22 + j, 44, c.fhalo, ti, t1=(c.t1[2 + j % 2] if 't1' in DBG else None))
            s.op("act", lambda e, yg=yg: e.activation(out=yg[:], in_=yg[:], func=AF.Silu), reads=[yg], writes=[yg])
            s.op("pool", lambda e, yg=yg, yu=yu, j=j: e.tensor_tensor(out=c.act[:, j, :], in0=yg[:], in1=yu[:], op=ALU.mult),
                 reads=[yg, yu], writes=[c.act])
            if j == 3 and pending is not None:
                toks.append(pending())
                pending = None
        if ti + 1 < NT:
            xt_next = load_x(c, src, ti + 1, NT)
        lst = ln_begin(c)
        for m in range(8):
            wd = c.wd[wi % 3]
            wi += 1
            s.dma("sp", lambda e, wd=wd, m=m: e.dma_start(out=wd[:].rearrange("p j n -> p (j n)"), in_=c.wdb_d[m]), reads=[wdb[m]], writes=[wd])
            pd = bk[2 + 3 * (m % 2)] if 'b2' not in DBG else bk[4 + m % 2]
            for j in range(22):
                s.op("pe", lambda e, wd=wd, pd=pd, j=j: e.matmul(pd[:], lhsT=wd[:, j, :], rhs=c.act[:, j, :],
                                                                 start=(j == 0), stop=(j == 21)),
                     reads=[wd, c.act], writes=[pd])
            ln_chunk(c, lst, xt, m, pd[:], pd, bk[6], bk[7])
        ln_flush(c, lst)
        pending = (lambda xt=xt, ti=ti: ln_finish(c, xt, bk[6], bk[7], layer, 1, dst, ti))
    toks.append(pending())
    return toks


def ab_phase(c, layer, src, dst, NT):
    s = c.s
    j = layer // 2
    bk = c.banks
    psb = c.psb
    w_in = wres_new(c, "ab_w_in", 8, 3336)
    w_out = wres_new(c, "ab_w_out", 8, 1024)
    load_weight(c, w_in, c.ab_w_in[j], 3336, piece=834)
    load_weight(c, w_out, c.ab_w_out[j], 1024, piece=512)
    alloc_x(c, 2)
    cst = psb("cst", [128, 512])
    sel = psb("sel", [8, 8, 128])
    s.dma("sp", lambda e: e.dma_start(out=cst[:], in_=c.cst_in), writes=[cst])
    s.dma("sp", lambda e: e.dma_start(out=sel[:], in_=c.sel_in), writes=[sel])
    triU, onesf, ident, negm = cst[:, 0:128], cst[:, 128:256], cst[:, 256:384], cst[:, 384:512]
    negm4 = psb("negm4", [128, 4, 128])
    for q_ in range(4):
        s.op("dve", lambda e, q_=q_: e.tensor_copy(out=negm4[:, q_, :], in_=negm), reads=[cst], writes=[negm4])
    ones256 = psb("ones256", [128, 128], BF16)
    s.op("dve", lambda e: e.memset(ones256[:], 1.0 / 256), writes=[ones256])
    ones128 = psb("ones128", [128, 128], BF16)
    s.op("dve", lambda e: e.memset(ones128[:], 1.0 / 128), writes=[ones128])
    onesb = psb("onesb", [128, 64])
    s.op("dve", lambda e: e.memset(onesb[:], 1.0), writes=[onesb])
    eps_r = psb("eps_r", [128, 1])
    s.op("dve", lambda e: e.memset(eps_r[:], RMS_EPS), writes=[eps_r])
    dtb4 = psb("dtb4", [128, 4, 8])
    a4 = psb("a4", [128, 4, 8])
    o_dtb = BC_OFF["dtb%d" % j][0]
    o_al = BC_OFF["alog%d" % j][0]
    for q_ in range(4):
        s.op("dve", lambda e, q_=q_: e.tensor_copy(out=dtb4[:, q_, :], in_=c.bcp[:, o_dtb:o_dtb + 8]), reads=[c.bcp], writes=[dtb4])
        s.op("act", lambda e, q_=q_: e.activation(out=a4[:, q_, :], in_=c.bcp[:, o_al:o_al + 8], func=AF.Exp), reads=[c.bcp], writes=[a4])
    s.op("dve", lambda e: e.tensor_scalar(out=a4[:], in0=a4[:], scalar1=-1.0, scalar2=None, op0=ALU.mult), reads=[a4], writes=[a4])
    lb = psb("lb", [128, 4])
    oml = psb("oml", [128, 4])
    if j == 0:
        s.op("dve", lambda e: e.memset(lb[:], 0.0), writes=[lb])
    else:
        s.op("dve", lambda e: e.tensor_tensor(out=lb[:], in0=ppv(c, "hgl1"), in1=ppv(c, "hgl0"), op=ALU.subtract), reads=[c.pp], writes=[lb])
        s.op("act", lambda e: e.activation(out=lb[:], in_=lb[:], func=AF.Sigmoid), reads=[lb], writes=[lb])
    s.op("dve", lambda e: e.tensor_scalar(out=oml[:], in0=lb[:], scalar1=-1.0, scalar2=1.0, op0=ALU.mult, op1=ALU.add), reads=[lb], writes=[oml])
    shalo = [psb("shalo%d" % i, [128, 6, 3]) for i in range(2)]
    for h_ in shalo:
        s.op("dve", lambda e, h_=h_: e.memset(h_[:], 0.0), writes=[h_])
    S32 = psb("S32", [128, 256])
    s.op("dve", lambda e: e.memset(S32[:], 0.0), writes=[S32])
    Spad = psb("Spad", [128, 8, 64], BF16)
    s.op("dve", lambda e: e.memset(Spad[:], 0.0), writes=[Spad])
    HS32 = [psb("HS32_%d" % h, [128, 128]) for h in range(4)]
    HSb = [psb("HSb_%d" % h, [128, 128], BF16) for h in range(4)]
    for h in range(4):
        s.op("dve", lambda e, h=h: e.memset(HS32[h][:], 0.0), writes=[HS32[h]])
        s.op("pool", lambda e, h=h: e.memset(HSb[h][:], 0.0), writes=[HSb[h]])
    BTpad = [psb("BTpad%d" % g, [128, TT], BF16) for g in range(2)]
    for g in range(2):
        s.op("pool", lambda e, g=g: e.memset(BTpad[g][:], 0.0), writes=[BTpad[g]])
    vt = [psb("vt%d" % i, [128, 512], BF16) for i in range(8)]
    for i in range(8):
        s.op("pool", lambda e, i=i: e.memset(vt[i][:], 0.0), writes=[vt[i]])
    attb = [psb("attb%d" % i, [128, 64], BF16) for i in range(2)]
    ktk = [psb("ktk%d" % i, [128, 128], BF16) for i in range(2)]
    for i in range(2):
        s.op("pool", lambda e, i=i: e.memset(attb[i][:], 0.0), writes=[attb[i]])
        s.op("pool", lambda e, i=i: e.memset(ktk[i][:], 0.0), writes=[ktk[i]])
    zs = psb("zs", [128, 4, TT], BF16)
    xs32 = psb("xs32", [128, 4, TT])
    B32 = psb("B32", [128, TT])
    CTb = psb("CTb", [128, TT], BF16)
    yc = psb("yc", [128, TT])
    dt = psb("dt", [128, 32])
    adt = psb("adt", [128, 32])
    cstok = psb("cstok", [128, 32])
    ncstok = psb("ncstok", [128, 32])
    csT = psb("csT", [8, 512])
    ncsT = psb("ncsT", [8, 512])
    ecsT = psb("ecsT", [8, 512])
    dstate = psb("dstate", [128, 32])
    dec = psb("dec", [128, 32])
    dec2 = psb("dec2", [128, 4, 4])
    Dm = [psb("Dm%d" % i, [128, 512]) for i in range(2)]
    scT = psb("scT", [128, 8, 128], BF16)
    CTs = psb("CTs", [128, 8, 128], BF16)
    xc = psb("xc_", [128, 512], BF16)
    xcd = psb("xcd", [128, 512], BF16)
    Btok = psb("Btok", [128, 128], BF16)
    ysb = psb("ysb", [128, 4, TT])
    yv2 = [psb("yv%d" % i, [128, TT]) for i in range(2)]
    yv = [yv2[0], yv2[1], yv2[0], yv2[1]]
    sqb = [psb("sqb%d" % i, [128, TT], BF16) for i in range(2)]
    ymix = psb("ymix", [128, 8, TT], BF16)
    q32 = AliasBuf(ysb, ysb[:, 0, :], "q32")
    k32 = AliasBuf(ysb, ysb[:, 1, :], "k32")
    lg = AliasBuf(ysb, ysb[:, 2, :], "lg")
    rr = yc
    bc = AliasBuf(ysb, ysb[:, 3, :], "bc")
    e1 = yc
    ex = B32
    qp = psb("qp", [128, TT], BF16)
    kp = psb("kp", [128, TT], BF16)
    qpp = psb("qpp", [128, TT], BF16)
    kppp = Dm[0]
    hdec = psb("hdec", [128, 8])
    hgs = Dm[1]
    o32 = q32
    d_off = PP_OFF["sd%d" % j][0]
    snw_off = PP_OFF["snw%d" % j][0]
    hnw = ppv(c, "hnw%d" % j)
    toks = []
    xt_next = load_x(c, src, 0, NT)
    pending = None
    for ti in range(NT):
        xt = xt_next
        rhs = [c.xb[:, k, :] for k in range(8)]
        for ch in range(4):
            ps = bk[ch % 2]
            mm(c, ps[:], ps, w_in, ch * 128, (ch + 1) * 128, rhs, [c.xb])
            s.op("act", lambda e, ps=ps, ch=ch: e.activation(out=zs[:, ch, :], in_=ps[:], func=AF.Silu), reads=[ps], writes=[zs])
        if pending is not None:
            toks.append(pending())
            pending = None
        for ch in range(6):
            ps = bk[ch % 2]
            mm(c, ps[:], ps, w_in, 512 + ch * 128, 512 + (ch + 1) * 128, rhs, [c.xb])
            conv_chunk(c, ps[:], ps, yc, 4, "scw%d" % j, "scb%d" % j, ch, 6, shalo, ti)
            if ch < 4:
                s.op("act", lambda e, ch=ch: e.activation(out=xs32[:, ch, :], in_=yc[:], func=AF.Silu), reads=[yc], writes=[xs32])
            elif ch == 4:
                s.op("act", lambda e: e.activation(out=B32[:], in_=yc[:], func=AF.Silu), reads=[yc], writes=[B32])
                s.op("pool", lambda e: e.tensor_copy(out=BTpad[0][0:64, :], in_=B32[0:64, :]), reads=[B32], writes=[BTpad[0]])
                s.op("pool", lambda e: e.tensor_copy(out=BTpad[1][64:128, :], in_=B32[64:128, :]), reads=[B32], writes=[BTpad[1]])
            else:
                s.op("act", lambda e: e.activation(out=CTb[:], in_=yc[:], func=AF.Silu), reads=[yc], writes=[CTb])
        ps = bk[0]
        wb = w_in.bufs(1280, 1288)
        for cc in range(4):
            for k in range(8):
                s.op("pe", lambda e, cc=cc, k=k, ps=ps: e.matmul(ps[:, cc * 8:(cc + 1) * 8], lhsT=c.xb[:, k, cc * 128:(cc + 1) * 128],
                                                               rhs=w_in.h[:, k, 1280:1288], start=(k == 0), stop=(k == 7)),
                     reads=wb + [w_in.rd, c.xb], writes=[ps])
        s.op("dve", lambda e, ps=ps: e.tensor_tensor(out=dt[:], in0=ps[:, 0:32], in1=dtb4[:].rearrange("p a b -> p (a b)"), op=ALU.add),
             reads=[ps, dtb4], writes=[dt])
        s.op("act", lambda e: e.activation(out=dt[:], in_=dt[:], func=AF.Exp), reads=[dt], writes=[dt])
        s.op("act", lambda e: e.activation(out=dt[:], in_=dt[:], func=AF.Ln, bias=1.0), reads=[dt], writes=[dt])
        s.op("dve", lambda e: e.tensor_tensor(out=adt[:], in0=dt[:], in1=a4[:].rearrange("p a b -> p (a b)"), op=ALU.mult), reads=[dt, a4], writes=[adt])
        ps = bk[1]
        s.op("pe", lambda e, ps=ps: e.matmul(ps[:, 0:32], lhsT=triU, rhs=adt[:], start=True, stop=True), reads=[cst, adt], writes=[ps])
        s.op("pe", lambda e, ps=ps: e.matmul(ps[:, 32:64], lhsT=onesf, rhs=adt[:], start=True, stop=True), reads=[cst, adt], writes=[ps])
        s.op("act", lambda e, ps=ps: e.activation(out=cstok[:], in_=ps[:, 0:32], func=AF.Copy), reads=[ps], writes=[cstok])
        s.op("act", lambda e, ps=ps: e.activation(out=ncstok[:], in_=ps[:, 0:32], func=AF.Copy, scale=-1.0), reads=[ps], writes=[ncstok])
        s.op("dve", lambda e, ps=ps: e.tensor_tensor(out=dstate[:], in0=ps[:, 32:64], in1=cstok[:], op=ALU.subtract), reads=[ps, cstok], writes=[dstate])
        s.op("act", lambda e: e.activation(out=dstate[:], in_=dstate[:], func=AF.Exp), reads=[dstate], writes=[dstate])
        s.op("act", lambda e, ps=ps: e.activation(out=dec[:], in_=ps[:, 32:64], func=AF.Exp), reads=[ps], writes=[dec])
        dv = dec[:].rearrange("p (a b) -> p a b", a=4)
        s.op("dve", lambda e: e.tensor_copy(out=dec2[0:64, :, :], in_=dv[0:64, :, 0:4]), reads=[dec], writes=[dec2])
        s.op("dve", lambda e: e.tensor_copy(out=dec2[64:128, :, :], in_=dv[64:128, :, 4:8]), reads=[dec], writes=[dec2])
        ps = bk[2]
        for cc in range(4):
            s.op("pe", lambda e, ps=ps, cc=cc: e.transpose(ps[0:8, cc * 128:(cc + 1) * 128], cstok[:, cc * 8:(cc + 1) * 8], ident), reads=[cstok, cst], writes=[ps])
        s.op("act", lambda e, ps=ps: e.activation(out=csT[:], in_=ps[0:8, :], func=AF.Copy), reads=[ps], writes=[csT])
        s.op("act", lambda e, ps=ps: e.activation(out=ncsT[:], in_=ps[0:8, :], func=AF.Copy, scale=-1.0), reads=[ps], writes=[ncsT])
        s.op("act", lambda e, ps=ps: e.activation(out=ecsT[:], in_=ps[0:8, :], func=AF.Exp), reads=[ps], writes=[ecsT])
        for cc in range(4):
            cols = slice(cc * 128, (cc + 1) * 128)
            for half in range(2):
                dps = bk[2 + half]
                eps_ = bk[4 + half]
                for hq_ in range(4):
                    i = half * 4 + hq_
                    s.op("pe", lambda e, dps=dps, hq_=hq_, i=i, cols=cols: e.matmul(dps[:, hq_ * 128:(hq_ + 1) * 128], lhsT=sel[:, i, :], rhs=csT[:, cols],
                                                                        start=True, stop=False), reads=[sel, csT], writes=[dps])
                    s.op("pe", lambda e, dps=dps, hq_=hq_, i=i, cols=cols: e.matmul(dps[:, hq_ * 128:(hq_ + 1) * 128], lhsT=ncsT[:, cols], rhs=sel[:, i, :],
                                                                        start=False, stop=True), reads=[sel, ncsT], writes=[dps])
                    s.op("pe", lambda e, eps_=eps_, hq_=hq_, i=i, cols=cols: e.matmul(eps_[:, hq_ * 128:(hq_ + 1) * 128], lhsT=sel[:, i, :], rhs=ecsT[:, cols],
                                                                          start=True, stop=True), reads=[sel, ecsT], writes=[eps_])
                dm = Dm[half]
                s.op("dve", lambda e, dps=dps, dm=dm: e.tensor_tensor(out=dm[:], in0=dps[:], in1=negm4[:].rearrange("p a b -> p (a b)"), op=ALU.add),
                     reads=[dps, negm4], writes=[dm])
                s.op("act", lambda e, dm=dm: e.activation(out=dm[:], in_=dm[:], func=AF.Exp), reads=[dm], writes=[dm])
                s.op("dve", lambda e, eps_=eps_, half=half, cols=cols: e.tensor_tensor(
                    out=CTs[:, half * 4:(half + 1) * 4, :], in0=eps_[:].rearrange("p (a b) -> p a b", a=4),
                    in1=CTb[:, cols].unsqueeze(1).to_broadcast([128, 4, 128]), op=ALU.mult), reads=[eps_, CTb], writes=[CTs])
            for g in range(2):
                gps = bk[6 + g]
                for r_ in range(4):
                    s.op("pe", lambda e, gps=gps, r_=r_, g=g, cols=cols: e.matmul(gps[:, r_ * 128:(r_ + 1) * 128], lhsT=BTpad[g][:, cols], rhs=CTb[:, cols],
                                                                               start=True, stop=True), reads=[BTpad[g], CTb], writes=[gps])
                s.op("dve", lambda e, gps=gps, g=g: e.tensor_tensor(out=scT[:, g * 4:(g + 1) * 4, :], in0=gps[:].rearrange("p (a b) -> p a b", a=4),
                                                                   in1=Dm[g][:].rearrange("p (a b) -> p a b", a=4), op=ALU.mult),
                     reads=[gps, Dm[g]], writes=[scT])
            xps = bk[0]
            for ch in range(4):
                s.op("pe", lambda e, xps=xps, ch=ch, cols=cols: e.transpose(xps[:, ch * 128:(ch + 1) * 128], xs32[:, ch, cols], ident),
                     reads=[xs32, cst], writes=[xps])
            s.op("dve", lambda e, xps=xps, cc=cc: e.tensor_tensor(out=xc[:].rearrange("p (h q) -> p h q", h=8), in0=xps[:].rearrange("p (h q) -> p h q", h=8),
                                                               in1=dt[:, cc * 8:(cc + 1) * 8].unsqueeze(2).to_broadcast([128, 8, 64]), op=ALU.mult),
                 reads=[xps, dt], writes=[xc])
            s.op("pool", lambda e, cc=cc: e.tensor_tensor(out=xcd[:].rearrange("p (h q) -> p h q", h=8), in0=xc[:].rearrange("p (h q) -> p h q", h=8),
                                                        in1=dstate[:, cc * 8:(cc + 1) * 8].unsqueeze(2).to_broadcast([128, 8, 64]), op=ALU.mult),
                 reads=[xc, dstate], writes=[xcd])
            bps = bk[1]
            s.op("pe", lambda e, bps=bps, cols=cols: e.transpose(bps[:, 0:128], B32[:, cols], ident), reads=[B32, cst], writes=[bps])
            s.op("act", lambda e, bps=bps: e.activation(out=Btok[:], in_=bps[:, 0:128], func=AF.Copy), reads=[bps], writes=[Btok])
            yps = bk[6]
            for h in range(8):
                oap = yps[(h % 2) * 64:(h % 2) * 64 + 64, (h // 2) * 128:(h // 2) * 128 + 128]
                s.op("pe", lambda e, oap=oap, h=h: e.matmul(oap, lhsT=xc[:, h * 64:(h + 1) * 64], rhs=scT[:, h, :], start=True, stop=False),
                     reads=[xc, scT], writes=[yps])
                s.op("pe", lambda e, oap=oap, h=h: e.matmul(oap, lhsT=Spad[:, h, :], rhs=CTs[:, h, :], start=False, stop=True),
                     reads=[Spad, CTs], writes=[yps])
            s.op("act", lambda e, yps=yps, cols=cols: e.activation(out=ysb[:, :, cols], in_=yps[:].rearrange("p (a b) -> p a b", a=4), func=AF.Copy),
                 reads=[yps], writes=[ysb])
            sps = bk[7]
            for g in range(2):
                s.op("pe", lambda e, sps=sps, g=g: e.matmul(sps[g * 64:(g + 1) * 64, 0:256], lhsT=Btok[:, g * 64:(g + 1) * 64], rhs=xcd[:, g * 256:(g + 1) * 256],
                                                           start=True, stop=True), reads=[Btok, xcd], writes=[sps])
            s.op("dve", lambda e, cc=cc: e.tensor_tensor(out=S32[:].rearrange("p (a b) -> p a b", a=4), in0=S32[:].rearrange("p (a b) -> p a b", a=4),
                                                       in1=dec2[:, cc, :].unsqueeze(2).to_broadcast([128, 4, 64]), op=ALU.mult), reads=[S32, dec2], writes=[S32])
            s.op("dve", lambda e, sps=sps: e.tensor_tensor(out=S32[:], in0=S32[:], in1=sps[:, 0:256], op=ALU.add), reads=[S32, sps], writes=[S32])
            s.op("act", lambda e: e.activation(out=Spad[0:64, 0:4, :], in_=S32[0:64, :].rearrange("p (a b) -> p a b", a=4), func=AF.Copy), reads=[S32], writes=[Spad])
            s.op("act", lambda e: e.activation(out=Spad[64:128, 4:8, :], in_=S32[64:128, :].rearrange("p (a b) -> p a b", a=4), func=AF.Copy), reads=[S32], writes=[Spad])
        for g in range(2):
            mps = bk[4 + g]
            for q_ in range(2):
                ch = 2 * g + q_
                y_ = yv[ch]
                s.op("dve", lambda e, ch=ch, y_=y_: e.scalar_tensor_tensor(out=y_[:], in0=xs32[:, ch, :], scalar=c.pp[:, d_off + ch:d_off + ch + 1],
                                                                          in1=ysb[:, ch, :], op0=ALU.mult, op1=ALU.add), reads=[xs32, ysb, c.pp], writes=[y_])
                s.op("pool", lambda e, ch=ch, y_=y_: e.tensor_tensor(out=y_[:], in0=y_[:], in1=zs[:, ch, :], op=ALU.mult), reads=[y_, zs], writes=[y_])
                sq = sqb[q_]
                s.op("act", lambda e, y_=y_, sq=sq: e.activation(out=sq[:], in_=y_[:], func=AF.Square), reads=[y_], writes=[sq])
                s.op("pe", lambda e, mps=mps, sq=sq, q_=q_: e.matmul(mps[:], lhsT=ones256[:], rhs=sq[:], start=(q_ == 0), stop=(q_ == 1)),
                     reads=[ones256, sq], writes=[mps])
            s.op("act", lambda e, mps=mps: e.activation(out=rr[:], in_=mps[:], func=AF.Sqrt, bias=eps_r[:]), reads=[mps, eps_r], writes=[rr])
            s.op("dve", lambda e: e.reciprocal(out=rr[:], in_=rr[:]), reads=[rr], writes=[rr])
            for q_ in range(2):
                ch = 2 * g + q_
                y_ = yv[ch]
                s.op("dve", lambda e, ch=ch, y_=y_: e.scalar_tensor_tensor(out=ymix[:, ch, :], in0=y_[:], scalar=c.pp[:, snw_off + ch:snw_off + ch + 1],
                                                                          in1=rr[:], op0=ALU.mult, op1=ALU.mult), reads=[y_, rr, c.pp], writes=[ymix])
        for cc in range(8):
            ps = bk[cc % 2]
            wb = w_in.bufs(2312, 2824)
            for k in range(8):
                s.op("pe", lambda e, cc=cc, k=k, ps=ps: e.matmul(ps[0:64, :], lhsT=c.xb[:, k, cc * 64:(cc + 1) * 64], rhs=w_in.h[:, k, 2312:2824],
                                                               start=(k == 0), stop=(k == 7)), reads=wb + [w_in.rd, c.xb], writes=[ps])
            s.op("act", lambda e, cc=cc, ps=ps: e.activation(out=vt[cc][0:64, :], in_=ps[0:64, :], func=AF.Copy), reads=[ps], writes=[vt[cc]])
        for hd in range(4):
            ps = bk[0]
            mm(c, ps[:], ps, w_in, 1288 + hd * 128, 1288 + (hd + 1) * 128, rhs, [c.xb])
            s.op("act", lambda e, ps=ps: e.activation(out=q32[:], in_=ps[:], func=AF.Silu), reads=[ps], writes=[q32])
            ps = bk[1]
            mm(c, ps[:], ps, w_in, 1800 + hd * 128, 1800 + (hd + 1) * 128, rhs, [c.xb])
            s.op("act", lambda e, ps=ps: e.activation(out=lg[:], in_=ps[:], func=AF.Sigmoid), reads=[ps], writes=[lg])
            s.op("dve", lambda e, hd=hd: e.tensor_scalar(out=lg[:], in0=lg[:], scalar1=oml[:, hd:hd + 1], scalar2=lb[:, hd:hd + 1], op0=ALU.mult, op1=ALU.add),
                 reads=[lg, oml, lb], writes=[lg])
            s.op("dve", lambda e: e.tensor_scalar(out=k32[:], in0=lg[:], scalar1=-1.0, scalar2=1.0, op0=ALU.mult, op1=ALU.add), reads=[lg], writes=[k32])
            s.op("act", lambda e: e.activation(out=lg[:], in_=lg[:], func=AF.Ln), reads=[lg], writes=[lg])
            ps = bk[2]
            mm(c, ps[:], ps, w_in, 2824 + hd * 128, 2824 + (hd + 1) * 128, rhs, [c.xb])
            s.op("act", lambda e, ps=ps: e.activation(out=hgs[:], in_=ps[:], func=AF.Silu), reads=[ps], writes=[hgs])
            if hd == 3 and ti + 1 < NT:
                xt_next = load_x(c, src, ti + 1, NT)
            for cc in range(8):
                s.op("dve", lambda e, cc=cc: e.tensor_tensor_scan(out=bc[:, cc * 64:(cc + 1) * 64], data0=onesb[:], data1=lg[:, cc * 64:(cc + 1) * 64],
                                                                  initial=0.0, op0=ALU.mult, op1=ALU.add), reads=[onesb, lg], writes=[bc])
            bc3 = bc[:].rearrange("p (a b) -> p a b", a=8)
            s.op("dve", lambda e: e.tensor_tensor(out=e1[:].rearrange("p (a b) -> p a b", a=8), in0=bc3, in1=bc3[:, :, 31:32].to_broadcast([128, 8, 64]), op=ALU.subtract),
                 reads=[bc], writes=[e1])
            s.op("act", lambda e: e.activation(out=ex[:], in_=e1[:], func=AF.Exp), reads=[e1], writes=[ex])
            s.op("dve", lambda e: e.tensor_tensor(out=qp[:], in0=q32[:], in1=ex[:], op=ALU.mult), reads=[q32, ex], writes=[qp])
            s.op("act", lambda e: e.activation(out=ex[:], in_=e1[:], func=AF.Exp, scale=-1.0), reads=[e1], writes=[ex])
            s.op("dve", lambda e: e.tensor_tensor(out=kp[:], in0=k32[:], in1=ex[:], op=ALU.mult), reads=[k32, ex], writes=[kp])
            s.op("act", lambda e: e.activation(out=ex[:], in_=bc[:], func=AF.Exp), reads=[bc], writes=[ex])
            s.op("dve", lambda e: e.tensor_tensor(out=qpp[:], in0=q32[:], in1=ex[:], op=ALU.mult), reads=[q32, ex], writes=[qpp])
            s.op("act", lambda e: e.activation(out=hdec[:], in_=bc3[:, :, 63], func=AF.Exp), reads=[bc], writes=[hdec])
            s.op("dve", lambda e: e.tensor_tensor(out=e1[:].rearrange("p (a b) -> p a b", a=8), in0=bc3[:, :, 63:64].to_broadcast([128, 8, 64]), in1=bc3, op=ALU.subtract),
                 reads=[bc], writes=[e1])
            s.op("act", lambda e: e.activation(out=ex[:], in_=e1[:], func=AF.Exp), reads=[e1], writes=[ex])
            s.op("dve", lambda e: e.tensor_tensor(out=kppp[:], in0=k32[:], in1=ex[:], op=ALU.mult), reads=[k32, ex], writes=[kppp])
            ops_ = bk[3]
            for cc in range(8):
                cols = slice(cc * 64, (cc + 1) * 64)
                aps = bk[4 + cc % 2]
                s.op("pe", lambda e, aps=aps, cols=cols: e.matmul(aps[0:64, 0:64], lhsT=kp[:, cols], rhs=qp[:, cols], start=True, stop=True),
                     reads=[kp, qp], writes=[aps])
                ab_ = attb[cc % 2]
                s.op("dve", lambda e, aps=aps, ab_=ab_: e.tensor_tensor(out=ab_[0:64, :], in0=aps[0:64, 0:64], in1=c.masks[0:64, 0, 0, 0:64], op=ALU.mult),
                     reads=[aps, c.masks], writes=[ab_])
                s.op("pe", lambda e, ops_=ops_, cols=cols, cc=cc, hd=hd, ab_=ab_: e.matmul(ops_[:, cols], lhsT=vt[cc][:, hd * 128:(hd + 1) * 128], rhs=ab_[:],
                                                                                    start=True, stop=False), reads=[vt[cc], ab_], writes=[ops_])
                s.op("pe", lambda e, ops_=ops_, cols=cols, hd=hd: e.matmul(ops_[:, cols], lhsT=HSb[hd][:], rhs=qpp[:, cols], start=False, stop=True),
                     reads=[HSb[hd], qpp], writes=[ops_])
                tps = bk[6 + cc % 2]
                s.op("pe", lambda e, tps=tps, cols=cols: e.transpose(tps[0:64, 0:128], kppp[:, cols], ident), reads=[kppp, cst], writes=[tps])
                kk = ktk[cc % 2]
                s.op("act", lambda e, tps=tps, kk=kk: e.activation(out=kk[0:64, :], in_=tps[0:64, 0:128], func=AF.Copy), reads=[tps], writes=[kk])
                s.op("pe", lambda e, tps=tps, kk=kk, cc=cc, hd=hd: e.matmul(tps[:, 128:256], lhsT=kk[:], rhs=vt[cc][:, hd * 128:(hd + 1) * 128], start=True, stop=True),
                     reads=[kk, vt[cc]], writes=[tps])
                s.op("dve", lambda e, tps=tps, cc=cc, hd=hd: e.scalar_tensor_tensor(out=HS32[hd][:], in0=HS32[hd][:], scalar=hdec[:, cc:cc + 1], in1=tps[:, 128:256],
                                                                                 op0=ALU.mult, op1=ALU.add), reads=[HS32[hd], hdec, tps], writes=[HS32[hd]])
                s.op("act", lambda e, hd=hd: e.activation(out=HSb[hd][:], in_=HS32[hd][:], func=AF.Copy), reads=[HS32[hd]], writes=[HSb[hd]])
            s.op("act", lambda e, ops_=ops_: e.activation(out=o32[:], in_=ops_[:], func=AF.Copy), reads=[ops_], writes=[o32])
            sq = sqb[hd % 2]
            s.op("act", lambda e, ops_=ops_, sq=sq: e.activation(out=sq[:], in_=ops_[:], func=AF.Square), reads=[ops_], writes=[sq])
            mps = bk[4]
            s.op("pe", lambda e, mps=mps, sq=sq: e.matmul(mps[:], lhsT=ones128[:], rhs=sq[:], start=True, stop=True), reads=[ones128, sq], writes=[mps])
            s.op("act", lambda e, mps=mps: e.activation(out=rr[:], in_=mps[:], func=AF.Sqrt, bias=eps_r[:]), reads=[mps, eps_r], writes=[rr])
            s.op("dve", lambda e: e.reciprocal(out=rr[:], in_=rr[:]), reads=[rr], writes=[rr])
            s.op("dve", lambda e: e.scalar_tensor_tensor(out=o32[:], in0=o32[:], scalar=hnw, in1=rr[:], op0=ALU.mult, op1=ALU.mult), reads=[o32, rr, c.pp], writes=[o32])
            s.op("dve", lambda e, hd=hd: e.tensor_tensor(out=ymix[:, 4 + hd, :], in0=o32[:], in1=hgs[:], op=ALU.mult), reads=[o32, hgs], writes=[ymix])
        lst = ln_begin(c)
        yr = [ymix[:, k, :] for k in range(8)]
        for m in range(8):
            pd = bk[4 + m % 2]
            mm(c, pd[:], pd, w_out, m * 128, (m + 1) * 128, yr, [ymix])
            ln_chunk(c, lst, xt, m, pd[:], pd, bk[6], bk[7])
        ln_flush(c, lst)
        pending = (lambda xt=xt, ti=ti: ln_finish(c, xt, bk[6], bk[7], layer, 0, dst, ti))
    toks.append(pending())
    return toks


DBG = set()


def cd_phase(c, layer, src, dst, NT):
    s = c.s
    j = layer // 2
    bk = c.banks
    psb = c.psb
    w_in = wres_new(c, "cd_w_in", 8, 1792)
    w_out = wres_new(c, "cd_w_out", 8, 1024)
    load_weight(c, w_in, c.cd_w_in[j], 1792, piece=896)
    load_weight(c, w_out, c.cd_w_out[j], 1024, piece=512)
    alloc_x(c, 2)
    qT = psb("qT", [128, 4, TT], BF16)
    KP = [[psb("KP%d%d" % (hh_, g_), [128, 128 + TT], BF16) for g_ in range(2)] for hh_ in range(2)]
    for hh_ in range(2):
        for g_ in range(2):
            s.op("pool", lambda e, b_=KP[hh_][g_]: e.memset(b_[:], 0.0), writes=[KP[hh_][g_]])
    vtok = psb("vtok", [128, 5, 128], BF16)
    PT = [psb("PT%d" % i, [128, 2, 512], BF16) for i in range(2)]
    Eb = [psb("Eb%d" % i, [128, 512], BF16) for i in range(2)]
    ymix = psb("ymix", [128, 8, TT], BF16)
    rden = psb("rden", [128, 256])
    sexp = psb("sexp", [128, 4])
    bd = psb("bd", [128, 2, 4, 128], BF16)
    cneg = psb("cneg", [128, 4])
    hst = psb("hst", [128, 4])
    rhalo = [psb("rhalo%d" % i, [128, 4, 3]) for i in range(2)]
    xc2 = [psb("xc%d" % i, [128, TT]) for i in range(2)]
    xcb2 = [psb("xcb%d" % i, [128, TT], BF16) for i in range(2)]
    rt2 = [[psb("rt%d_%d" % (p_, i), [128, TT]) for i in range(8)] for p_ in range(2)]
    s.op("act", lambda e: e.activation(out=sexp[:], in_=ppv(c, "sinkpp%d" % j), func=AF.Exp), reads=[c.pp], writes=[sexp])
    s.op("act", lambda e: e.activation(out=cneg[:], in_=ppv(c, "rlam%d" % j), func=AF.Exp, scale=-1.0), reads=[c.pp], writes=[cneg])
    s.op("act", lambda e: e.activation(out=cneg[:], in_=cneg[:], func=AF.Ln, bias=1.0), reads=[cneg], writes=[cneg])
    s.op("dve", lambda e: e.tensor_scalar(out=cneg[:], in0=cneg[:], scalar1=-8.0, scalar2=None, op0=ALU.mult), reads=[cneg], writes=[cneg])
    s.op("dve", lambda e: e.memset(hst[:], 0.0), writes=[hst])
    for h_ in rhalo:
        s.op("dve", lambda e, h_=h_: e.memset(h_[:], 0.0), writes=[h_])
    s.op("dve", lambda e: e.memset(bd[:], 0.0), writes=[bd])
    for w_ in range(2 if 'nobd' not in DBG else 0):
        for blk in range(8):
            o = (blk % 2) * 64
            s.dma("pool", lambda e, w_=w_, blk=blk, o=o: e.dma_start(out=bd[o:o + 64, w_, blk // 2, o:o + 64],
                                                                   in_=c.rg_w[j, w_, blk]), writes=[bd])
    ba = PP_OFF["rba%d" % j][0]
    bx = PP_OFF["rbx%d" % j][0]
    toks = []
    xt_next = load_x(c, src, 0, NT)
    pending = None
    for ti in range(NT):
        xt = xt_next
        rhs = [c.xb[:, k, :] for k in range(8)]
        for qc in range(4):
            ps = bk[qc % 2]
            mm(c, ps[:], ps, w_in, qc * 128, (qc + 1) * 128, rhs, [c.xb])
            s.op("act", lambda e, ps=ps, qc=qc: e.activation(out=qT[:, qc, :], in_=ps[:], func=AF.Copy), reads=[ps], writes=[qT])
        ps = bk[0]
        mm(c, ps[:], ps, w_in, 512, 640, rhs, [c.xb])
        s.op("act", lambda e, ps=ps: e.activation(out=KP[0][0][0:64, 128:128 + TT], in_=ps[0:64, :], func=AF.Copy), reads=[ps], writes=[KP[0][0]])
        s.op("act", lambda e, ps=ps: e.activation(out=KP[1][1][64:128, 128:128 + TT], in_=ps[64:128, :], func=AF.Copy), reads=[ps], writes=[KP[1][1]])
        ps = bk[1]
        mm(c, ps[0:64, :], ps, w_in, 576, 640, rhs, [c.xb])
        mm(c, ps[64:128, :], ps, w_in, 512, 576, rhs, [c.xb])
        s.op("act", lambda e, ps=ps: e.activation(out=KP[0][1][0:64, 128:128 + TT], in_=ps[0:64, :], func=AF.Copy), reads=[ps], writes=[KP[0][1]])
        s.op("act", lambda e, ps=ps: e.activation(out=KP[1][0][64:128, 128:128 + TT], in_=ps[64:128, :], func=AF.Copy), reads=[ps], writes=[KP[1][0]])
        ps = bk[0]
        wb = w_in.bufs(640, 768)
        for bi in range(4):
            for k in range(8):
                s.op("pe", lambda e, bi=bi, k=k, ps=ps: e.matmul(ps[:, bi * 128:(bi + 1) * 128], lhsT=c.xb[:, k, bi * 128:(bi + 1) * 128],
                                                               rhs=w_in.h[:, k, 640:768], start=(k == 0), stop=(k == 7)),
                     reads=wb + [w_in.rd, c.xb], writes=[ps])
        s.op("act", lambda e, ps=ps: e.activation(out=vtok[:, 1:5, :], in_=ps[:].rearrange("p (b v) -> p b v", b=4), func=AF.Copy),
             reads=[ps], writes=[vtok])
        if pending is not None:
            toks.append(pending())
            pending = None
        it = 0
        for bi in range(4 if 'noswa' not in DBG else 0):
            gb = ti * 4 + bi
            for g in range(2):
                kbs = [(0, 128 + bi * 128, 1 + bi)]
                if gb > 0:
                    kbs.append((1, bi * 128, bi))
                pt = PT[it % 2]
                for kbi, (which, kc0, slot) in enumerate(kbs):
                    sc = bk[2 + kbi]
                    for jh in range(4):
                        h = 4 * g + jh
                        hh = h % 2
                        qc = h // 2
                        Kx = KP[hh][g]
                        s.op("pe", lambda e, sc=sc, jh=jh, hh=hh, qc=qc, Kx=Kx, kc0=kc0, bi=bi: e.matmul(
                            sc[:, jh * 128:(jh + 1) * 128], lhsT=Kx[:, kc0:kc0 + 128],
                            rhs=qT[:, qc, bi * 128:(bi + 1) * 128], start=True, stop=True),
                            reads=[Kx, qT], writes=[sc])
                    eb = Eb[kbi]
                    s.op("act", lambda e, sc=sc, eb=eb: e.activation(out=eb[:], in_=sc[:], func=AF.Exp, scale=0.125), reads=[sc], writes=[eb])
                    s.op("dve", lambda e, eb=eb, pt=pt, kbi=kbi, which=which: e.tensor_tensor(
                        out=pt[:, kbi, :], in0=eb[:], in1=c.masks[:, which, :, :].rearrange("p j q -> p (j q)"), op=ALU.mult),
                        reads=[eb, c.masks], writes=[pt])
                ob = bk[4 + it % 2]
                nk = len(kbs)
                for den in range(2 if 'nopv' not in DBG else 0):
                    for par in range(2):
                        for kbi, (which, kc0, slot) in enumerate(kbs):
                            lhsT = (c.ones64[:, :] if den else vtok[:, slot, g * 64:(g + 1) * 64])
                            rhs_ap = pt[:, kbi, :].rearrange("p (jj par q) -> p jj par q", jj=2, par=2)[:, :, par, :]
                            out_ap = ob[par * 64:(par + 1) * 64, den * 256:(den + 1) * 256].rearrange("p (jj q) -> p jj q", jj=2)
                            s.op("pe", lambda e, lhsT=lhsT, rhs_ap=rhs_ap, out_ap=out_ap, kbi=kbi, nk=nk: e.matmul(
                                out_ap, lhsT=lhsT, rhs=rhs_ap, start=(kbi == 0), stop=(kbi == nk - 1)),
                                reads=[vtok, c.ones64, pt], writes=[ob])
                if 'nofin' in DBG:
                    it += 1
                    continue
                for jj in range(2):
                    sx = sexp[:, g * 2 + jj:g * 2 + jj + 1]
                    s.op("dve", lambda e, ob=ob, jj=jj, sx=sx: e.tensor_scalar(out=rden[:, jj * 128:(jj + 1) * 128],
                                                                             in0=ob[:, 256 + jj * 128:256 + (jj + 1) * 128],
                                                                             scalar1=sx, scalar2=None, op0=ALU.add),
                         reads=[ob, sexp], writes=[rden])
                s.op("dve", lambda e: e.reciprocal(out=rden[:], in_=rden[:]), reads=[rden], writes=[rden])
                s.op("dve", lambda e, ob=ob, g=g, bi=bi: e.tensor_tensor(
                    out=ymix[:, 2 * g:2 * g + 2, bi * 128:(bi + 1) * 128], in0=ob[:, 0:256].rearrange("p (jj q) -> p jj q", jj=2),
                    in1=rden[:].rearrange("p (jj q) -> p jj q", jj=2), op=ALU.mult), reads=[ob, rden], writes=[ymix])
                it += 1
        s.op("pool", lambda e: e.tensor_copy(out=vtok[:, 0, :], in_=vtok[:, 4, :]), reads=[vtok], writes=[vtok])
        for hh_ in range(2):
            for g_ in range(2):
                s.op("pool", lambda e, b_=KP[hh_][g_]: e.tensor_copy(out=b_[:, 0:128], in_=b_[:, TT:TT + 128]), reads=[KP[hh_][g_]], writes=[KP[hh_][g_]])
        for ch in range(4 if 'norg' not in DBG else 0):
            p_ = ch % 2
            psg, psx, psr, psi = bk[p_], bk[2 + p_], bk[4 + p_], bk[6 + p_]
            xc, xcb = xc2[p_], xcb2[p_]
            tr, ti_, ta, tm, tu, th, tg1, tg2 = rt2[p_]
            mm(c, psg[:], psg, w_in, 768 + ch * 128, 768 + (ch + 1) * 128, rhs, [c.xb])
            mm(c, psx[:], psx, w_in, 1280 + ch * 128, 1280 + (ch + 1) * 128, rhs, [c.xb])
            s.op("act", lambda e, psg=psg, tg1=tg1: e.activation(out=tg1[:], in_=psg[:], func=AF.Square), reads=[psg], writes=[tg1])
            s.op("act", lambda e, psg=psg, tg2=tg2: e.activation(out=tg2[:], in_=psg[:], func=AF.Copy), reads=[psg], writes=[tg2])
            conv_chunk(c, psx[:], psx, xc, 4, "rcw%d" % j, "rcb%d" % j, ch, 4, rhalo, ti)
            s.op("pool", lambda e, xc=xc, xcb=xcb: e.tensor_copy(out=xcb[:], in_=xc[:]), reads=[xc], writes=[xcb])
            s.op("pe", lambda e, ch=ch, psr=psr, xcb=xcb: e.matmul(psr[:], lhsT=bd[:, 0, ch, :], rhs=xcb[:], start=True, stop=True), reads=[bd, xcb], writes=[psr])
            s.op("pe", lambda e, ch=ch, psi=psi, xcb=xcb: e.matmul(psi[:], lhsT=bd[:, 1, ch, :], rhs=xcb[:], start=True, stop=True), reads=[bd, xcb], writes=[psi])
            s.op("pool", lambda e, tg1=tg1: e.tensor_scalar(out=tg1[:], in0=tg1[:], scalar1=0.044715, scalar2=1.0, op0=ALU.mult, op1=ALU.add),
                 reads=[tg1], writes=[tg1])
            s.op("pool", lambda e, tg1=tg1, tg2=tg2: e.tensor_tensor(out=tg1[:], in0=tg1[:], in1=tg2[:], op=ALU.mult), reads=[tg1, tg2], writes=[tg1])
            s.op("act", lambda e, ch=ch, psr=psr, tr=tr: e.activation(out=tr[:], in_=psr[:], func=AF.Sigmoid, bias=c.pp[:, ba + ch:ba + ch + 1]),
                 reads=[psr, c.pp], writes=[tr])
            s.op("act", lambda e, ch=ch, psi=psi, ti_=ti_: e.activation(out=ti_[:], in_=psi[:], func=AF.Sigmoid, bias=c.pp[:, bx + ch:bx + ch + 1]),
                 reads=[psi, c.pp], writes=[ti_])
            s.op("act", lambda e, tg1=tg1: e.activation(out=tg1[:], in_=tg1[:], func=AF.Sigmoid, scale=2.0 * math.sqrt(2.0 / math.pi)),
                 reads=[tg1], writes=[tg1])
            s.op("act", lambda e, ch=ch, ta=ta, tr=tr: e.activation(out=ta[:], in_=tr[:], func=AF.Exp, scale=cneg[:, ch:ch + 1]), reads=[tr, cneg], writes=[ta])
            s.op("pool", lambda e, tm=tm, ta=ta: e.tensor_tensor(out=tm[:], in0=ta[:], in1=ta[:], op=ALU.mult), reads=[ta], writes=[tm])
            s.op("act", lambda e, tm=tm: e.activation(out=tm[:], in_=tm[:], func=AF.Sqrt, scale=-1.0, bias=1.0), reads=[tm], writes=[tm])
            s.op("pool", lambda e, tu=tu, ti_=ti_, xc=xc: e.tensor_tensor(out=tu[:], in0=ti_[:], in1=xc[:], op=ALU.mult), reads=[ti_, xc], writes=[tu])
            s.op("dve", lambda e, tu=tu, tm=tm: e.tensor_tensor(out=tu[:], in0=tu[:], in1=tm[:], op=ALU.mult), reads=[tu, tm], writes=[tu])
            s.op("dve", lambda e, ch=ch, th=th, ta=ta, tu=tu: e.tensor_tensor_scan(out=th[:], data0=ta[:], data1=tu[:], initial=hst[:, ch:ch + 1],
                                                                              op0=ALU.mult, op1=ALU.add), reads=[ta, tu, hst], writes=[th])
            s.op("act", lambda e, ch=ch, th=th: e.activation(out=hst[:, ch:ch + 1], in_=th[:, TT - 1:TT], func=AF.Copy), reads=[th], writes=[hst])
            s.op("pool", lambda e, tg1=tg1, tg2=tg2: e.tensor_tensor(out=tg1[:], in0=tg1[:], in1=tg2[:], op=ALU.mult), reads=[tg1, tg2], writes=[tg1])
            s.op("dve", lambda e, ch=ch, th=th, tg1=tg1: e.tensor_tensor(out=ymix[:, 4 + ch, :], in0=th[:], in1=tg1[:], op=ALU.mult),
                 reads=[th, tg1], writes=[ymix])
        if ti + 1 < NT:
            xt_next = load_x(c, src, ti + 1, NT)
        lst = ln_begin(c)
        yr = [ymix[:, k, :] for k in range(8)]
        for m in range(8):
            pd = bk[4 + m % 2]
            mm(c, pd[:], pd, w_out, m * 128, (m + 1) * 128, yr, [ymix])
            ln_chunk(c, lst, xt, m, pd[:], pd, bk[6], bk[7])
        ln_flush(c, lst)
        pending = (lambda xt=xt, ti=ti: ln_finish(c, xt, bk[6], bk[7], layer, 0, dst, ti))
    toks.append(pending())
    return toks


def make_masks():
    s_ = np.arange(128)[:, None]
    t_ = np.arange(128)[None, :]
    m = np.zeros((128, 2, 4, 128), np.float32)
    m[:, 0] = (s_ <= t_).astype(np.float32)[:, None, :]
    m[:, 1] = (s_ > t_).astype(np.float32)[:, None, :]
    return m


def make_cst():
    a = np.arange(128)
    triU = (a[:, None] <= a[None, :]).astype(np.float32)
    ones = np.ones((128, 128), np.float32)
    ident = np.eye(128, dtype=np.float32)
    negm = np.where(a[None, :] < a[:, None], NEG, 0.0).astype(np.float32)
    return np.ascontiguousarray(np.concatenate([triU, ones, ident, negm], axis=1))


def make_sel():
    m = np.zeros((8, 8, 128), np.float32)
    for i in range(8):
        m[i, i, :] = 1.0
    return m


_CACHE = {}


def run_cores(inputs, S, layer_list, phases, n_active, xT_list):
    key = (S, tuple(layer_list), tuple(phases))
    if key not in _CACHE:
        _CACHE[key] = build_program(S, layer_list, phases)
    nc = _CACHE[key]
    pp, bc = pack_params(inputs)
    rg_w = np.stack([np.asarray(inputs["rg_wa"], np.float32), np.asarray(inputs["rg_wx"], np.float32)], axis=1)
    base = {
        "ab_w_in": np.asarray(inputs["ab_w_in"], np.float32), "ab_w_out": np.asarray(inputs["ab_w_out"], np.float32),
        "cd_w_in": np.asarray(inputs["cd_w_in"], np.float32), "cd_w_out": np.asarray(inputs["cd_w_out"], np.float32),
        "ffn_w_up": np.asarray(inputs["ffn_w_up"], np.float32), "ffn_w_down": np.asarray(inputs["ffn_w_down"], np.float32),
        "pp": pp, "bcp": bc, "rg_w": np.ascontiguousarray(rg_w), "masks": make_masks(), "cst": make_cst(), "sel": make_sel(),
    }
    in_maps = []
    for i in range(n_active):
        m = dict(base)
        m["xT"] = xT_list[i]
        in_maps.append(m)
    res = run_bass_kernel_spmd(nc, in_maps, core_ids=list(range(n_active)))
    return [r["outT"] for r in res.results]


def kernel(**inputs):
    x = np.asarray(inputs["x"], np.float32)
    B, S, _ = x.shape
    xT = [np.ascontiguousarray(x[b].T).reshape(8, 128, S) for b in range(B)]
    outs = run_cores(inputs, S, [0, 1, 2, 3], ("mix", "ffn"), B, xT)
    out = np.stack([outs[b].reshape(1024, S).T for b in range(B)], axis=0)
    return np.ascontiguousarray(out).astype(np.float32)
```
